# Optimizing a Trainium2 kernel written in Bass

```python
import jax, jax.numpy as jnp
from jax import lax
import numpy as np

D_MODEL = 2048
BATCH = 8
SEQ = 4096
DEPTH = 2

MIX_WIDTH = D_MODEL
MLA_HEADS = D_MODEL // 256
QK_NOPE_DIM = 128
QK_ROPE_DIM = 64
QK_HEAD_DIM = QK_NOPE_DIM + QK_ROPE_DIM
V_HEAD_DIM = 128
Q_LORA_RANK = 512
KV_LORA_RANK = 256
ATTN_WIDTH = MLA_HEADS * V_HEAD_DIM
GM_WIDTH = MIX_WIDTH - ATTN_WIDTH
GM_GROUPS = D_MODEL // 256
GM_GROUP_DIM = GM_WIDTH // GM_GROUPS
CHUNK = 128
D_FF = 128 * ((8 * D_MODEL // 3 + 127) // 128)
PLE_DIM = 256
ROPE_BASE = 10000.0
EPS = 1e-6
Q_BLOCK = 128
IN_COLS = Q_LORA_RANK + KV_LORA_RANK + QK_ROPE_DIM + 2 * GM_WIDTH
SPLITS = (Q_LORA_RANK,
          Q_LORA_RANK + KV_LORA_RANK,
          Q_LORA_RANK + KV_LORA_RANK + QK_ROPE_DIM,
          Q_LORA_RANK + KV_LORA_RANK + QK_ROPE_DIM + GM_WIDTH)

kernel_name = "hybrid_mla_gmlp_macaron_ple"


def rms_norm(x, g):
    xf = x.astype(jnp.float32)
    y = xf * lax.rsqrt(jnp.mean(xf * xf, axis=-1, keepdims=True) + EPS)
    return (y * g.astype(jnp.float32)).astype(x.dtype)


def swiglu(x, w1, w3, w2):
    return (jax.nn.silu(x @ w1) * (x @ w3)) @ w2


def rope_tables(positions):
    inv_freq = ROPE_BASE ** (-jnp.arange(0, QK_ROPE_DIM, 2, dtype=jnp.float32) / QK_ROPE_DIM)
    ang = positions.astype(jnp.float32)[..., None] * inv_freq
    return jnp.cos(ang)[:, :, None, :], jnp.sin(ang)[:, :, None, :]


def apply_rope(x, cos, sin):
    x1, x2 = jnp.split(x.astype(jnp.float32), 2, axis=-1)
    out = jnp.concatenate([x1 * cos - x2 * sin, x2 * cos + x1 * sin], axis=-1)
    return out.astype(x.dtype)


def causal_block_attention(q, k, v):
    b, s, h, dqk = q.shape
    nb = s // Q_BLOCK
    scale = dqk ** -0.5
    qb = q.reshape(b, nb, Q_BLOCK, h, dqk).transpose(1, 0, 2, 3, 4)
    key_pos = jnp.arange(s)
    neg = jnp.finfo(jnp.float32).min

    def one_block(args):
        q_blk, blk = args
        q_pos = blk * Q_BLOCK + jnp.arange(Q_BLOCK)
        scores = jnp.einsum('bqhd,bkhd->bhqk', q_blk, k,
                            preferred_element_type=jnp.float32) * scale
        mask = key_pos[None, :] <= q_pos[:, None]
        scores = jnp.where(mask[None, None], scores, neg)
        probs = jax.nn.softmax(scores, axis=-1).astype(v.dtype)
        return jnp.einsum('bhqk,bkhd->bqhd', probs, v)

    out = lax.map(one_block, (qb, jnp.arange(nb)))
    return out.transpose(1, 0, 2, 3, 4).reshape(b, s, h, v.shape[-1])


def mla_mixer(c_q, c_kv, k_rope_raw, cos, sin, q_a_norm, w_uq, kv_a_norm, w_ukv, q_norm, k_norm):
    b, s, _ = c_q.shape
    q = (rms_norm(c_q, q_a_norm) @ w_uq).reshape(b, s, MLA_HEADS, QK_HEAD_DIM)
    kv = (rms_norm(c_kv, kv_a_norm) @ w_ukv).reshape(b, s, MLA_HEADS, QK_NOPE_DIM + V_HEAD_DIM)
    k_nope, v = jnp.split(kv, [QK_NOPE_DIM], axis=-1)
    k_rope = jnp.broadcast_to(k_rope_raw[:, :, None, :], (b, s, MLA_HEADS, QK_ROPE_DIM))
    k = jnp.concatenate([k_nope, k_rope], axis=-1)
    q = rms_norm(q, q_norm)
    k = rms_norm(k, k_norm)
    q = jnp.concatenate([q[..., :QK_NOPE_DIM], apply_rope(q[..., QK_NOPE_DIM:], cos, sin)], axis=-1)
    k = jnp.concatenate([k[..., :QK_NOPE_DIM], apply_rope(k[..., QK_NOPE_DIM:], cos, sin)], axis=-1)
    return causal_block_attention(q, k, v).reshape(b, s, ATTN_WIDTH)


def gmlp_mixer(u, v, v_norm, w_s, b_s):
    b, s, _ = u.shape
    u = jax.nn.gelu(u)
    v = rms_norm(jax.nn.gelu(v), v_norm)
    vc = v.reshape(b, s // CHUNK, CHUNK, GM_GROUPS, GM_GROUP_DIM)
    tril = jnp.tril(jnp.ones((CHUNK, CHUNK), dtype=bool))
    w_causal = jnp.where(tril[None], w_s, jnp.zeros_like(w_s))
    gate = jnp.einsum('gts,bcsgd->bctgd', w_causal, vc) + b_s.T[None, None, :, :, None]
    return u * gate.reshape(b, s, GM_WIDTH)


def setup_inputs(seed: int = 0) -> dict:
    key = jax.random.key(seed)
    ks = jax.random.split(key, 32)
    f32 = jnp.float32

    def w(k, shape, fan_in):
        return jax.random.normal(k, shape, f32) * fan_in ** -0.5

    def g(k, shape):
        return 1.0 + 0.05 * jax.random.normal(k, shape, f32)

    L = DEPTH
    offsets = jax.random.randint(ks[2], (BATCH, 1), 0, 1024, dtype=jnp.int32)
    positions = (offsets + jnp.arange(SEQ, dtype=jnp.int32)[None, :]).astype(jnp.int32)
    return {
        "x": jax.random.normal(ks[0], (BATCH, SEQ, D_MODEL), f32),
        "p": jax.random.normal(ks[1], (DEPTH, BATCH, SEQ, PLE_DIM), f32),
        "positions": positions,
        "ffn_a_norm": g(ks[3], (L, D_MODEL)),
        "ffn_a_w1": w(ks[4], (L, D_MODEL, D_FF), D_MODEL),
        "ffn_a_w3": w(ks[5], (L, D_MODEL, D_FF), D_MODEL),
        "ffn_a_w2": w(ks[6], (L, D_FF, D_MODEL), D_FF),
        "mix_norm": g(ks[7], (L, D_MODEL)),
        "w_in": w(ks[8], (L, D_MODEL, IN_COLS), D_MODEL),
        "q_a_norm": g(ks[9], (L, Q_LORA_RANK)),
        "w_uq": w(ks[10], (L, Q_LORA_RANK, MLA_HEADS * QK_HEAD_DIM), Q_LORA_RANK),
        "kv_a_norm": g(ks[11], (L, KV_LORA_RANK)),
        "w_ukv": w(ks[12], (L, KV_LORA_RANK, MLA_HEADS * (QK_NOPE_DIM + V_HEAD_DIM)), KV_LORA_RANK),
        "q_norm": g(ks[13], (L, QK_HEAD_DIM)),
        "k_norm": g(ks[14], (L, QK_HEAD_DIM)),
        "gm_v_norm": g(ks[15], (L, GM_WIDTH)),
        "gm_ws": w(ks[16], (L, GM_GROUPS, CHUNK, CHUNK), CHUNK),
        "gm_bs": 1.0 + 0.1 * jax.random.normal(ks[17], (L, GM_GROUPS, CHUNK), f32),
        "attn_out_norm": g(ks[18], (L, ATTN_WIDTH)),
        "gm_out_norm": g(ks[19], (L, GM_WIDTH)),
        "w_out": w(ks[20], (L, MIX_WIDTH, D_MODEL), MIX_WIDTH),
        "ffn_b_norm": g(ks[21], (L, D_MODEL)),
        "ffn_b_w1": w(ks[22], (L, D_MODEL, D_FF), D_MODEL),
        "ffn_b_w3": w(ks[23], (L, D_MODEL, D_FF), D_MODEL),
        "ffn_b_w2": w(ks[24], (L, D_FF, D_MODEL), D_FF),
        "ple_gate_norm": g(ks[25], (L, D_MODEL)),
        "w_ple_gate": w(ks[26], (L, D_MODEL, D_MODEL), D_MODEL),
        "w_ple": w(ks[27], (L, PLE_DIM, D_MODEL), PLE_DIM),
        "ple_norm": g(ks[28], (L, D_MODEL)),
    }


def reference(x, p, positions, ffn_a_norm, ffn_a_w1, ffn_a_w3, ffn_a_w2, mix_norm, w_in,
              q_a_norm, w_uq, kv_a_norm, w_ukv, q_norm, k_norm, gm_v_norm, gm_ws, gm_bs,
              attn_out_norm, gm_out_norm, w_out, ffn_b_norm, ffn_b_w1, ffn_b_w3, ffn_b_w2,
              ple_gate_norm, w_ple_gate, w_ple, ple_norm):
    cos, sin = rope_tables(positions)
    h = x
    for i in range(DEPTH):
        h = h + 0.5 * swiglu(rms_norm(h, ffn_a_norm[i]), ffn_a_w1[i], ffn_a_w3[i], ffn_a_w2[i])
        z = rms_norm(h, mix_norm[i]) @ w_in[i]
        c_q, c_kv, k_rope_raw, u, v = jnp.split(z, SPLITS, axis=-1)
        a_out = mla_mixer(c_q, c_kv, k_rope_raw, cos, sin, q_a_norm[i], w_uq[i],
                          kv_a_norm[i], w_ukv[i], q_norm[i], k_norm[i])
        g_out = gmlp_mixer(u, v, gm_v_norm[i], gm_ws[i], gm_bs[i])
        mixed = jnp.concatenate([rms_norm(a_out, attn_out_norm[i]),
                                 rms_norm(g_out, gm_out_norm[i])], axis=-1)
        h = h + mixed @ w_out[i]
        h = h + 0.5 * swiglu(rms_norm(h, ffn_b_norm[i]), ffn_b_w1[i], ffn_b_w3[i], ffn_b_w2[i])
        e = rms_norm(p[i] @ w_ple[i], ple_norm[i])
        gate = jax.nn.sigmoid(rms_norm(h, ple_gate_norm[i]) @ w_ple_gate[i])
        h = h + gate * e
    return h
```

```python
import numpy as np
from contextlib import ExitStack
import concourse.bass as bass
import concourse.mybir as mybir
from concourse.bass_utils import run_bass_kernel_spmd

F32 = mybir.dt.float32
BF16 = mybir.dt.bfloat16
I32 = mybir.dt.int32
AF = mybir.ActivationFunctionType
ALU = mybir.AluOpType
AX = mybir.AxisListType

D = 2048
DFF = 5504
NH = 8
QL = 512
KVL = 256
ROPE = 64
GMW = 1024
INC = 2880
PLE = 256
EPS = 1e-6


class Lane:
    def __init__(self, name, step):
        self.name = name
        self.step = step
        self.sem = None
        self.total = 0
        self.ops = []


class Op:
    __slots__ = ("eng", "fn", "deps", "lane", "signal", "count", "idx")


ENGS = ("pe", "act", "dve", "pool", "sp")


_UID = [0]
POOL = {"es": None, "sems": {}}


def uid():
    _UID[0] += 1
    return _UID[0]


class Sched:
    def __init__(self, nc, es):
        self.uid = uid()
        self.nc = nc
        self.es = es
        self.ops = {e: [] for e in ENGS}
        self.lanes = {}
        self.res_w = {}
        self.res_r = {}
        self.n = 0
        for e in ENGS:
            self.lanes[e] = Lane(e, 1)

    def lane(self, name):
        if name not in self.lanes:
            self.lanes[name] = Lane(name, 16)
        return self.lanes[name]

    def add(self, eng, fn, reads=(), writes=(), dma=None):
        op = Op()
        op.eng = eng
        op.fn = fn
        op.signal = False
        op.count = None
        op.idx = self.n
        self.n += 1
        op.lane = self.lane(dma) if dma is not None else self.lanes[eng]
        if dma is not None:
            op.signal = True
        deps = {}
        for k in reads:
            w = self.res_w.get(k)
            if w is not None:
                deps[w.idx] = w
        for k in writes:
            w = self.res_w.get(k)
            if w is not None:
                deps[w.idx] = w
            for r in self.res_r.get(k, ()):
                deps[r.idx] = r
        for k in reads:
            self.res_r.setdefault(k, []).append(op)
        for k in writes:
            self.res_w[k] = op
            self.res_r[k] = []
        dl = []
        for d in deps.values():
            if d is op:
                continue
            if d.lane is op.lane and op.eng == "pe" and dma is None:
                continue
            d.signal = True
            dl.append(d)
        op.deps = dl
        op.lane.ops.append(op)
        self.ops[eng].append(op)
        return op

    def emit(self):
        nc = self.nc
        for ln in self.lanes.values():
            if ln.ops:
                ln.ops[-1].signal = True
        lanes = [ln for ln in self.lanes.values() if ln.ops]
        nd = 0
        for ln in lanes:
            if ln.step == 1:
                gname = "e_" + ln.name
            else:
                gname = f"d{nd}"
                nd += 1
            if gname not in POOL["sems"]:
                POOL["sems"][gname] = [POOL["es"].enter_context(nc.semaphore("g_" + gname)), 0]
            ent = POOL["sems"][gname]
            ln.sem = ent[0]
            c = ent[1]
            ln.base = c
            for op in ln.ops:
                if op.signal:
                    c += ln.step
                op.count = c
            ln.total = c
            ent[1] = c
            assert c < 60000, (gname, c)

        def run(eng_obj, ename):
            waited = {}
            for op in self.ops[ename]:
                need = {}
                for d in op.deps:
                    ln = d.lane
                    c = d.count
                    if ln.step == 16:
                        c = max([o.count for o in ln.ops if o.idx < op.idx] + [ln.base])
                    if c <= ln.base:
                        continue
                    if need.get(ln.name, 0) < c:
                        need[ln.name] = c
                for lname, c in need.items():
                    if waited.get(lname, 0) >= c:
                        continue
                    waited[lname] = c
                    eng_obj.wait_ge(self.lanes[lname].sem, c)
                ins = op.fn(eng_obj)
                if op.signal:
                    ins.then_inc(op.lane.sem, op.lane.step)
            for ln in lanes:
                if waited.get(ln.name, 0) < ln.total:
                    eng_obj.wait_ge(ln.sem, ln.total)

        with nc.Block() as block:
            @block.tensor
            def _(e):
                run(e, "pe")

            @block.scalar
            def _(e):
                run(e, "act")

            @block.vector
            def _(e):
                run(e, "dve")

            @block.gpsimd
            def _(e):
                run(e, "pool")

            @block.sync
            def _(e):
                run(e, "sp")


def bcast_row(ap1d, nparts):
    return ap1d.partition_broadcast(nparts)


class Ctx:
    pass


def load_consts(nc, es, cx):
    cx.ident = es.enter_context(nc.sbuf_tensor("ident", [128, 128], BF16))
    cx.identf = es.enter_context(nc.sbuf_tensor("identf", [128, 128], F32))
    cx.ones_bf = es.enter_context(nc.sbuf_tensor("ones_bf", [128, 128], BF16))
    cx.eps_col = es.enter_context(nc.sbuf_tensor("eps_col", [128, 1], F32))
    with ExitStack() as es2:
        S = Sched(nc, es2)

        def mk(e):
            e.memset(cx.identf[:], 0.0)
            return e.affine_select(out=cx.identf[:], in_=cx.identf[:], pattern=[[-1, 128]],
                                   compare_op=ALU.not_equal, fill=1.0, base=0,
                                   channel_multiplier=1)
        S.add("pool", mk, writes=["identf"])
        S.add("dve", lambda e: e.tensor_copy(out=cx.ident[:], in_=cx.identf[:]),
              reads=["identf"], writes=["ident"])
        S.add("dve", lambda e: e.memset(cx.ones_bf[:], 1.0), writes=["ones"])
        S.add("dve", lambda e: e.memset(cx.eps_col[:], EPS), writes=["eps"])
        S.emit()


def norm_tile(S, cx, xin, xn, ss, rstd, gB, junk, Dn, tag, x_key, xn_key):
    def sq(e):
        return e.activation(out=junk, in_=xin, func=AF.Square, accum_out=ss)
    S.add("act", sq, reads=[x_key], writes=[tag + "ss", xn_key])

    def rs(e):
        return e.activation(out=rstd, in_=ss, func=AF.Sqrt, scale=1.0 / Dn, bias=cx.eps_col[:])
    S.add("act", rs, reads=[tag + "ss"], writes=[tag + "rstd"])
    S.add("dve", lambda e: e.reciprocal(out=rstd, in_=rstd), reads=[tag + "rstd"],
          writes=[tag + "rstd"])

    def nm(e):
        return e.scalar_tensor_tensor(out=xn, in0=xin, scalar=rstd, in1=gB,
                                      op0=ALU.mult, op1=ALU.mult)
    S.add("dve", nm, reads=[x_key, tag + "rstd", "gB"], writes=[xn_key])


def stage_ffn(nc, cx, h_src, h_dst, g_norm, w1, w3, w2, NTOK, T, Dm=D, F=DFF):
    KC = Dm // 128
    FC = F // 128
    NS = T // 128
    NHF = T // 512
    NT = NTOK // T
    NG = Dm // 512
    FW = 256 if F % 256 == 0 else 128
    NFW = F // FW
    FPW = FW // 128
    W2G = 4
    with ExitStack() as es:
        S = Sched(nc, es)
        sb = lambda name, shape, dt: es.enter_context(nc.sbuf_tensor(f"{name}_{S.uid}", shape, dt))
        NXB = 2
        xin = [sb(f"xin{i}", [128, Dm], F32) for i in range(NXB)]
        xn = [sb(f"xn{i}", [128, Dm], BF16) for i in range(NXB)]
        ss = [sb(f"ss{i}", [128, 1], F32) for i in range(NXB)]
        rstd = [sb(f"rstd{i}", [128, 1], F32) for i in range(NXB)]
        gB = sb("gB", [128, Dm], F32)
        xnT = sb("xnT", [128, KC, T], BF16)
        hid = sb("hid", [128, FC, T], BF16)
        NWB = 2
        w1b = [sb(f"w1b{i}", [128, KC, FW], BF16) for i in range(NWB)]
        w3b = [sb(f"w3b{i}", [128, KC, FW], BF16) for i in range(NWB)]
        NW2 = 3
        w2b = [sb(f"w2b{i}", [128, W2G, 512], BF16) for i in range(NW2)]
        NHB = 8
        hb = [sb(f"hb{i}", [128, 512], F32) for i in range(NHB)]
        sil = [sb(f"sil{i}", [128, 512], F32) for i in range(2)]
        ps = es.enter_context(nc.psum_tensor(f"ps_{S.uid}", [128, 8, 512], F32))

        S.add("sp", lambda e: e.dma_start(out=gB[:], in_=bcast_row(g_norm, 128)),
              writes=["gB"], dma="gB")

        src_t = h_src.rearrange("(n p) d -> n p d", p=128)
        dst_t = h_dst.rearrange("(n p) d -> n p d", p=128)
        w1v = w1.rearrange("(kc p) f -> p kc f", p=128)
        w3v = w3.rearrange("(kc p) f -> p kc f", p=128)
        w2v = w2.rearrange("(fc p) d -> p fc d", p=128)

        xcnt = 0
        wcnt = 0
        w2cnt = 0
        hcnt = 0
        silc = 0
        tpb = 0
        for t in range(NT):
            for s in range(NS):
                b = xcnt % NXB
                xcnt += 1
                row = t * NS + s
                S.add("sp", lambda e, b=b, row=row: e.dma_start(out=xin[b][:], in_=src_t[row]),
                      writes=[f"xin{b}"], dma=f"xin{b}")
                norm_tile(S, cx, xin[b][:], xn[b][:], ss[b][:], rstd[b][:], gB[:], xn[b][:], Dm,
                          f"n{b}", f"xin{b}", f"xn{b}")
                for half in range(KC // 8):
                    bank = tpb % 8
                    tpb += 1
                    pst = ps[:, bank, :].bitcast(BF16)

                    def tr(e, b=b, half=half, pst=pst):
                        ins = None
                        for j in range(8):
                            kc = half * 8 + j
                            ins = e.transpose(out=pst[:, j * 128:(j + 1) * 128],
                                              in_=xn[b][:, kc * 128:(kc + 1) * 128],
                                              identity=cx.ident[:])
                        return ins
                    S.add("pe", tr, reads=[f"xn{b}", "ident"], writes=[f"ps{bank}"])

                    def ev(e, half=half, pst=pst, s=s):
                        return e.tensor_copy(
                            out=xnT[:, half * 8:(half + 1) * 8, s * 128:(s + 1) * 128],
                            in_=pst.rearrange("p (j c) -> p j c", j=8))
                    eng = "act" if (half % 2 == 0) else "dve"
                    if eng == "act":
                        def ev(e, half=half, pst=pst, s=s):
                            return e.copy(
                                out=xnT[:, half * 8:(half + 1) * 8, s * 128:(s + 1) * 128],
                                in_=pst.rearrange("p (j c) -> p j c", j=8))
                    S.add(eng, ev, reads=[f"ps{bank}"], writes=[f"xnT{s}"])
            for fw in range(NFW):
                wb = wcnt % NWB
                wcnt += 1
                S.add("pool", lambda e, wb=wb, fw=fw: e.dma_start(
                    out=w1b[wb][:], in_=w1v[:, :, fw * FW:(fw + 1) * FW]),
                    writes=[f"w1b{wb}"], dma=f"w1b{wb}")
                S.add("pool", lambda e, wb=wb, fw=fw: e.dma_start(
                    out=w3b[wb][:], in_=w3v[:, :, fw * FW:(fw + 1) * FW]),
                    writes=[f"w3b{wb}"], dma=f"w3b{wb}")
                for fi in range(FPW):
                    fc = fw * FPW + fi
                    par = fc % (8 // (2 * NHF)) if NHF <= 2 else 0
                    base = par * 2 * NHF
                    for mat in range(2):
                        wbuf = (w1b, w3b)[mat][wb]
                        banks = [base + mat * NHF + hf for hf in range(NHF)]

                        def mm(e, wbuf=wbuf, fi=fi, banks=banks):
                            ins = None
                            for kc in range(KC):
                                for hf in range(NHF):
                                    ins = e.matmul(out=ps[:, banks[hf], :],
                                                   lhsT=wbuf[:, kc, fi * 128:(fi + 1) * 128],
                                                   rhs=xnT[:, kc, hf * 512:(hf + 1) * 512],
                                                   start=(kc == 0), stop=(kc == KC - 1))
                            return ins
                        S.add("pe", mm,
                              reads=[("w1b", "w3b")[mat] + str(wb)] + [f"xnT{s}" for s in range(NS)],
                              writes=[f"ps{bk}" for bk in banks])
                    for hf in range(NHF):
                        b1 = base + hf
                        b3 = base + NHF + hf
                        sl = silc % 2
                        silc += 1
                        S.add("act", lambda e, b1=b1, sl=sl: e.activation(
                            out=sil[sl][:], in_=ps[:, b1, :], func=AF.Silu),
                            reads=[f"ps{b1}"], writes=[f"sil{sl}"])
                        S.add("dve", lambda e, b3=b3, sl=sl, fc=fc, hf=hf: e.tensor_tensor(
                            out=hid[:, fc, hf * 512:(hf + 1) * 512], in0=sil[sl][:],
                            in1=ps[:, b3, :], op=ALU.mult),
                            reads=[f"ps{b3}", f"sil{sl}"], writes=[f"hid{fc}"])
            assert NS <= 8
            for g in range(NG):
                for s in range(NS):
                    row = t * NS + s
                    S.add("sp", lambda e, s=s, row=row, g=g: e.dma_start(
                        out=hb[s][:], in_=src_t[row][:, g * 512:(g + 1) * 512]),
                        writes=[f"hb{s}"], dma=f"hbl{s}")
                for f0 in range(0, FC, W2G):
                    nf = min(W2G, FC - f0)
                    wb = w2cnt % NW2
                    w2cnt += 1
                    S.add("pool", lambda e, wb=wb, f0=f0, nf=nf, g=g: e.dma_start(
                        out=w2b[wb][:, 0:nf, :], in_=w2v[:, f0:f0 + nf, g * 512:(g + 1) * 512]),
                        writes=[f"w2b{wb}"], dma=f"w2b{wb}")

                    def mm2(e, wb=wb, f0=f0, nf=nf):
                        ins = None
                        for j in range(nf):
                            fc = f0 + j
                            for s in range(NS):
                                ins = e.matmul(out=ps[:, s, :],
                                               lhsT=hid[:, fc, s * 128:(s + 1) * 128],
                                               rhs=w2b[wb][:, j, :],
                                               start=(fc == 0), stop=(fc == FC - 1))
                        return ins
                    S.add("pe", mm2, reads=[f"w2b{wb}"] + [f"hid{f0 + j}" for j in range(nf)],
                          writes=[f"ps{s}" for s in range(NS)])
                for s in range(NS):
                    row = t * NS + s
                    hbi = s
                    S.add("dve", lambda e, hbi=hbi, s=s: e.scalar_tensor_tensor(
                        out=hb[hbi][:], in0=ps[:, s, :], scalar=0.5, in1=hb[hbi][:],
                        op0=ALU.mult, op1=ALU.add),
                        reads=[f"ps{s}", f"hb{hbi}"], writes=[f"hb{hbi}"])
                    S.add("sp", lambda e, hbi=hbi, row=row, g=g: e.dma_start(
                        out=dst_t[row][:, g * 512:(g + 1) * 512], in_=hb[hbi][:]),
                        reads=[f"hb{hbi}"], dma=f"hbs{hbi}")
        S.emit()


class Rot:
    def __init__(self, items):
        self.items = list(items)
        self.i = 0

    def next(self):
        v = self.items[self.i % len(self.items)]
        self.i += 1
        return v


def rstd_op(S, cx, ss, out, n, rk, wk, mul=None):
    S.add("act", lambda e: e.activation(out=out, in_=ss, func=AF.Sqrt, scale=1.0 / n,
                                        bias=cx.eps_col[:]), reads=rk, writes=wk)
    S.add("dve", lambda e: e.reciprocal(out=out, in_=out), reads=wk, writes=wk)


def norm_transpose(S, cx, src_row, xin, xn, ss, rstd, gB, xnT_dst, ps, banks, b, KC, dst_key, nb=0,
                   load=True):
    if load:
        S.add("sp", lambda e: e.dma_start(out=xin, in_=src_row), writes=[f"xin{b}"], dma=f"xin{b}")
    norm_tile(S, cx, xin, xn, ss, rstd, gB, xn, KC * 128, f"n{nb}", f"xin{b}", f"xn{nb}")
    for half in range((KC + 7) // 8):
        nk = min(8, KC - half * 8)
        bank = banks.next()
        pst = ps[:, bank, :].bitcast(BF16)

        def tr(e, half=half, pst=pst, nk=nk):
            ins = None
            for j in range(nk):
                kc = half * 8 + j
                ins = e.transpose(out=pst[:, j * 128:(j + 1) * 128],
                                  in_=xn[:, kc * 128:(kc + 1) * 128], identity=cx.ident[:])
            return ins
        S.add("pe", tr, reads=[f"xn{nb}", "ident"], writes=[f"ps{bank}"])
        dst = xnT_dst[:, half * 8:half * 8 + nk, :]
        src = pst[:, 0:nk * 128].rearrange("p (j c) -> p j c", j=nk)
        if half % 2 == 0:
            S.add("act", lambda e, dst=dst, src=src: e.copy(out=dst, in_=src),
                  reads=[f"ps{bank}"], writes=[dst_key])
        else:
            S.add("dve", lambda e, dst=dst, src=src: e.tensor_copy(out=dst, in_=src),
                  reads=[f"ps{bank}"], writes=[dst_key])


def load_w_resident(S, wtile, wsrc, KC, ncols, key, chunk=512):
    wv = wsrc.rearrange("(kc p) n -> p kc n", p=128)
    i = 0
    for c0 in range(0, ncols, chunk):
        c1 = min(ncols, c0 + chunk)
        S.add("pool", lambda e, c0=c0, c1=c1: e.dma_start(out=wtile[:, :, c0:c1],
                                                          in_=wv[:, :, c0:c1]),
              writes=[f"{key}{i}"], dma=f"{key}{i}")
        i += 1
    return [f"{key}{j}" for j in range(i)]


def stage_rope(nc, cx, pos_pm, inv_freq, sc, NTOK):
    NB = NTOK // 128
    PI = float(np.pi)
    with ExitStack() as es:
        S = Sched(nc, es)
        sb = lambda name, shape, dt: es.enter_context(nc.sbuf_tensor(f"{name}_{S.uid}", shape, dt))
        posi = sb("posi", [128, NB], I32)
        posf = sb("posf", [128, NB], F32)
        invf = sb("invf", [128, 32], F32)
        ang = sb("ang", [128, NB, 32], F32)
        arg = sb("arg", [128, NB, 32], F32)
        cs = sb("cs", [128, NB, 32], F32)
        sn = sb("sn", [128, NB, 32], F32)
        negpi = sb("negpi", [128, 1], F32)
        S.add("sp", lambda e: e.dma_start(out=posi[:], in_=pos_pm), writes=["posi"], dma="posi")
        S.add("sp", lambda e: e.dma_start(out=invf[:], in_=bcast_row(inv_freq, 128)),
              writes=["invf"], dma="invf")
        S.add("dve", lambda e: e.memset(negpi[:], -PI), writes=["negpi"])
        S.add("dve", lambda e: e.tensor_copy(out=posf[:], in_=posi[:]), reads=["posi"],
              writes=["posf"])
        S.add("dve", lambda e: e.tensor_tensor(
            out=ang[:], in0=posf[:].unsqueeze(2).to_broadcast([128, NB, 32]),
            in1=invf[:].unsqueeze(1).to_broadcast([128, NB, 32]), op=ALU.mult),
            reads=["posf", "invf"], writes=["ang"])
        ki = sb("ki", [128, NB, 32], I32)
        kf = sb("kf", [128, NB, 32], F32)
        C1 = 6.28125
        C2 = 2 * PI - C1
        for (shift, dstt, name) in ((0.0, sn, "sn"), (0.5 * PI, cs, "cs")):
            S.add("dve", lambda e, shift=shift: e.tensor_scalar(
                out=arg[:], in0=ang[:], scalar1=shift, scalar2=None, op0=ALU.add),
                reads=["ang"], writes=["arg"])
            S.add("dve", lambda e: e.tensor_scalar(
                out=kf[:], in0=arg[:], scalar1=1.0 / (2 * PI), scalar2=None, op0=ALU.mult),
                reads=["arg"], writes=["kf"])
            S.add("dve", lambda e: e.tensor_copy(out=ki[:], in_=kf[:]), reads=["kf"], writes=["ki"])
            S.add("dve", lambda e: e.tensor_copy(out=kf[:], in_=ki[:]), reads=["ki"], writes=["kf"])
            S.add("dve", lambda e: e.scalar_tensor_tensor(
                out=arg[:], in0=kf[:], scalar=-C1, in1=arg[:], op0=ALU.mult, op1=ALU.add),
                reads=["kf", "arg"], writes=["arg"])
            S.add("dve", lambda e: e.scalar_tensor_tensor(
                out=arg[:], in0=kf[:], scalar=-C2, in1=arg[:], op0=ALU.mult, op1=ALU.add),
                reads=["kf", "arg"], writes=["arg"])
            S.add("dve", lambda e: e.tensor_scalar(
                out=arg[:], in0=arg[:], scalar1=-3.141592, scalar2=3.141592, op0=ALU.max,
                op1=ALU.min), reads=["arg"], writes=["arg"])
            S.add("act", lambda e, dstt=dstt: e.activation(out=dstt[:], in_=arg[:], func=AF.Sin),
                  reads=["arg"], writes=[name])
        S.add("sp", lambda e: e.dma_start(out=sc["cos"], in_=cs[:]), reads=["cs"], dma="cst")
        S.add("sp", lambda e: e.dma_start(out=sc["sin"], in_=sn[:]), reads=["sn"], dma="snt")
        S.emit()


def stage_zg(nc, cx, h, W, sc, NTOK):
    NB = NTOK // 128
    T = 512
    NT = NTOK // T
    KC = 16
    GC = 0.044715
    GS = 1.5957691216057308
    with ExitStack() as es:
        S = Sched(nc, es)
        sb = lambda name, shape, dt: es.enter_context(nc.sbuf_tensor(f"{name}_{S.uid}", shape, dt))
        win = sb("win", [128, KC, INC], BF16)
        gB = sb("gB", [128, D], F32)
        gqa = sb("gqa", [128, 6], F32)
        gvB = sb("gvB", [128, GMW], F32)
        goB = sb("goB", [128, GMW], F32)
        wsT = sb("wsT", [128, 8, 128], BF16)
        bT = sb("bT", [128, 8], F32)
        stat = sb("stat", [128, NB, 8], F32)
        krr = sb("krr", [128, NB, 64], F32)
        xin = [sb(f"xin{i}", [128, D], F32) for i in range(2)]
        xn = sb("xn", [128, D], BF16)
        ss = sb("ss", [128, 1], F32)
        rstd = sb("rstd", [128, 1], F32)
        xnT = sb("xnT", [128, KC, T], BF16)
        cev = [sb(f"cev{i}", [128, T], BF16) for i in range(2)]
        junk = sb("junk", [128, 1024], BF16)
        tmp = [sb(f"tmp{i}", [128, 1024], F32) for i in range(2)]
        uraw = [sb(f"uraw{i}", [128, 1024], F32) for i in range(2)]
        vraw = [sb(f"vraw{i}", [128, 1024], F32) for i in range(2)]
        ug = sb("ug", [128, 1024], F32)
        vg = sb("vg", [128, 1024], F32)
        vn = sb("vn", [128, 1024], BF16)
        go = sb("go", [128, 1024], F32)
        gob = sb("gob", [128, 1024], BF16)
        goT = [sb(f"goT{i}", [128, 8, 128], BF16) for i in range(2)]
        wsf_v = tmp[0][:].rearrange("p (g s) -> p g s", g=8)
        wsb_v = gob[:].rearrange("p (g s) -> p g s", g=8)
        sst = sb("sst", [128, 4], F32)
        ps = es.enter_context(nc.psum_tensor(f"ps_{S.uid}", [128, 8, 512], F32))

        S.add("sp", lambda e: e.dma_start(out=gB[:], in_=bcast_row(W["mix_norm"], 128)),
              writes=["gB"], dma="gB")
        S.add("sp", lambda e: e.dma_start(out=gvB[:], in_=bcast_row(W["gm_v_norm"], 128)),
              writes=["gvB"], dma="gvB")
        S.add("sp", lambda e: e.dma_start(out=goB[:], in_=bcast_row(W["gm_out_norm"], 128)),
              writes=["goB"], dma="goB")
        S.add("sp", lambda e: e.dma_start(
            out=gqa[:, 0:4], in_=W["q_a_norm"].rearrange("(c p) -> p c", p=128),
            allow_slow_non_contiguous=True), writes=["gqa"], dma="gqa")
        S.add("sp", lambda e: e.dma_start(
            out=gqa[:, 4:6], in_=W["kv_a_norm"].rearrange("(c p) -> p c", p=128),
            allow_slow_non_contiguous=True), writes=["gqa"], dma="gqa")
        S.add("sp", lambda e: e.dma_start(
            out=bT[:], in_=W["gm_bs"].rearrange("g t -> t g"), allow_slow_non_contiguous=True),
            writes=["bT"], dma="bT")
        S.add("sp", lambda e: e.dma_start(out=wsf_v, in_=W["gm_ws"].rearrange("g t s -> t g s")),
              writes=["tmp0"], dma="wsf")
        S.add("pool", lambda e: e.affine_select(
            out=wsf_v, in_=wsf_v, pattern=[[0, 8], [-1, 128]], compare_op=ALU.is_ge, fill=0.0,
            base=0, channel_multiplier=1), reads=["tmp0"], writes=["tmp0"])
        S.add("dve", lambda e: e.tensor_copy(out=wsb_v, in_=wsf_v), reads=["tmp0"],
              writes=["gob"])

        def trw(e):
            pst = ps[:, 0, :].bitcast(BF16)
            ins = None
            for g in range(8):
                ins = e.transpose(out=pst[:, g * 128:(g + 1) * 128], in_=wsb_v[:, g, :],
                                  identity=cx.ident[:])
            return ins
        S.add("pe", trw, reads=["gob", "ident"], writes=["ps0"])
        S.add("dve", lambda e: e.tensor_copy(
            out=wsT[:], in_=ps[:, 0, :].bitcast(BF16).rearrange("p (g t) -> p g t", g=8)),
            reads=["ps0"], writes=["wsT"])
        wkeys = load_w_resident(S, win, W["w_in"], KC, INC, "win", chunk=576)
        wkey_of = lambda c0, c1: [f"win{c}" for c in range(c0 // 576, (c1 - 1) // 576 + 1)]

        h_t = h.rearrange("(n p) d -> n p d", p=128)
        tb = Rot([6, 7])
        xcnt = 0
        def phase0(t):
            nonlocal xcnt
            for s in range(4):
                b = xcnt % 2
                xcnt += 1
                blk = t * 4 + s
                norm_transpose(S, cx, h_t[blk], xin[b][:], xn[:], ss[:], rstd[:], gB[:],
                               xnT[:, :, s * 128:(s + 1) * 128], ps, tb, b, KC, f"xnT{s}")
        phase0(0)
        for t in range(NT):
            xkeys = [f"xnT{s}" for s in range(4)]
            for cc in range(6):
                bank = tb.next()

                def mmc(e, cc=cc, bank=bank):
                    ins = None
                    for kc in range(KC):
                        ins = e.matmul(out=ps[:, bank, :], lhsT=win[:, kc, cc * 128:(cc + 1) * 128],
                                       rhs=xnT[:, kc, :], start=(kc == 0), stop=(kc == KC - 1))
                    return ins
                S.add("pe", mmc, reads=xkeys + wkey_of(cc * 128, cc * 128 + 128),
                      writes=[f"ps{bank}"])
                cb = cc % 2
                S.add("act", lambda e, cc=cc, bank=bank, cb=cb: e.activation(
                    out=cev[cb][:], in_=ps[:, bank, :], func=AF.Copy, scale=gqa[:, cc:cc + 1]),
                    reads=[f"ps{bank}", "gqa"], writes=[f"cev{cb}"])
                dstT = sc["cqT"][cc] if cc < 4 else sc["ckvT"][cc - 4]
                S.add("sp", lambda e, dstT=dstT, cb=cb, t=t: e.dma_start(
                    out=dstT[:, t * T:(t + 1) * T], in_=cev[cb][:]),
                    reads=[f"cev{cb}"], dma=f"cevs{cb}")
            def a_mm(s, t=t):
                def mmt(e, c0, c1, banks):
                    ins = None
                    for kc in range(KC):
                        for i, bk in enumerate(banks):
                            a_ = c0 + i * 512
                            bnd = min(c1, a_ + 512)
                            ins = e.matmul(out=ps[:, bk, 0:bnd - a_],
                                           lhsT=xnT[:, kc, s * 128:(s + 1) * 128],
                                           rhs=win[:, kc, a_:bnd], start=(kc == 0),
                                           stop=(kc == KC - 1))
                    return ins
                S.add("pe", lambda e: mmt(e, 0, 832, (0, 1)),
                      reads=[f"xnT{s}"] + wkey_of(0, 832), writes=["ps0", "ps1"])
                S.add("pe", lambda e: mmt(e, 832, 1856, (2, 3)),
                      reads=[f"xnT{s}"] + wkey_of(832, 1856), writes=["ps2", "ps3"])
                S.add("pe", lambda e: mmt(e, 1856, 2880, (4, 5)),
                      reads=[f"xnT{s}"] + wkey_of(1856, 2880), writes=["ps4", "ps5"])

            def a_ev(s, t=t):
                blk = t * 4 + s
                par = blk % 2
                S.add("act", lambda e: e.copy(out=uraw[par][:].rearrange("p (a b) -> p a b", a=2),
                                              in_=ps[:, 2:4, :]),
                      reads=["ps2", "ps3"], writes=[f"uraw{par}"])
                S.add("act", lambda e: e.copy(out=vraw[par][:].rearrange("p (a b) -> p a b", a=2),
                                              in_=ps[:, 4:6, :]),
                      reads=["ps4", "ps5"], writes=[f"vraw{par}"])
                S.add("act", lambda e: e.activation(out=junk[:, 0:512], in_=ps[:, 0, :],
                                                    func=AF.Square, accum_out=sst[:, 0:1]),
                      reads=["ps0"], writes=["junk", "sst0"])
                S.add("act", lambda e: e.activation(out=junk[:, 0:256], in_=ps[:, 1, 0:256],
                                                    func=AF.Square, accum_out=sst[:, 1:2]),
                      reads=["ps1"], writes=["junk", "sst1"])
                S.add("act", lambda e: e.activation(
                    out=junk[:, 0:64], in_=ps[:, 1, 256:320], func=AF.Square,
                    accum_out=stat[:, blk, 2:3]), reads=["ps1"], writes=["junk", "stat"])
                S.add("dve", lambda e: e.tensor_copy(out=krr[:, blk, :], in_=ps[:, 1, 256:320]),
                      reads=["ps1"], writes=["krr"])
                rstd_op(S, cx, sst[:, 0:1], stat[:, blk, 0:1], QL, ["sst0"], ["stat"])
                rstd_op(S, cx, sst[:, 1:2], stat[:, blk, 1:2], KVL, ["sst1"], ["stat"])

            def b_chain(s, t=t):
                blk = t * 4 + s
                par = blk % 2
                for (raw, rk, dst, name, ti) in ((uraw[par], f"uraw{par}", ug, "ug", 0),
                                                 (vraw[par], f"vraw{par}", vg, "vg", 1)):
                    tm = tmp[ti][:]
                    xs = raw[:]
                    dv = dst[:]
                    tk = [f"tmp{ti}"]
                    S.add("act", lambda e, xs=xs, tm=tm: e.activation(out=tm, in_=xs, func=AF.Square),
                          reads=[rk], writes=tk)
                    S.add("dve", lambda e, tm=tm: e.tensor_scalar(
                        out=tm, in0=tm, scalar1=GC, scalar2=1.0, op0=ALU.mult, op1=ALU.add),
                        reads=tk, writes=tk)
                    S.add("dve", lambda e, tm=tm, xs=xs: e.tensor_tensor(
                        out=tm, in0=tm, in1=xs, op=ALU.mult), reads=tk + [rk], writes=tk)
                    S.add("act", lambda e, tm=tm: e.activation(out=tm, in_=tm, func=AF.Sigmoid,
                                                               scale=GS), reads=tk, writes=tk)
                    S.add("dve", lambda e, tm=tm, xs=xs, dv=dv: e.tensor_tensor(
                        out=dv, in0=tm, in1=xs, op=ALU.mult), reads=tk + [rk], writes=[name])
                S.add("act", lambda e: e.activation(out=junk[:], in_=vg[:], func=AF.Square,
                                                    accum_out=sst[:, 2:3]),
                      reads=["vg"], writes=["junk", "sst2"])
                rstd_op(S, cx, sst[:, 2:3], sst[:, 2:3], GMW, ["sst2"], ["sst2"])
                S.add("dve", lambda e: e.scalar_tensor_tensor(
                    out=vn[:], in0=vg[:], scalar=sst[:, 2:3], in1=gvB[:], op0=ALU.mult,
                    op1=ALU.mult), reads=["vg", "sst2", "gvB"], writes=["vn"])

                def mmg(e):
                    ins = None
                    for g in range(8):
                        ins = e.matmul(out=ps[:, 6 + g // 4, (g % 4) * 128:(g % 4 + 1) * 128],
                                       lhsT=wsT[:, g, :], rhs=vn[:, g * 128:(g + 1) * 128],
                                       start=True, stop=True)
                    return ins
                S.add("pe", mmg, reads=["vn", "wsT"], writes=["ps6", "ps7"])
                S.add("dve", lambda e: e.tensor_tensor(
                    out=go[:].rearrange("p (g d) -> p g d", g=8),
                    in0=ps[:, 6:8, :].rearrange("p a (g d) -> p (a g) d", g=4),
                    in1=bT[:].unsqueeze(2).to_broadcast([128, 8, 128]), op=ALU.add),
                    reads=["ps6", "ps7", "bT"], writes=["go"])
                S.add("dve", lambda e: e.tensor_tensor(out=go[:], in0=go[:], in1=ug[:],
                                                       op=ALU.mult),
                      reads=["go", "ug"], writes=["go"])
                S.add("act", lambda e: e.activation(out=junk[:], in_=go[:], func=AF.Square,
                                                    accum_out=sst[:, 3:4]),
                      reads=["go"], writes=["junk", "sst3"])
                rstd_op(S, cx, sst[:, 3:4], stat[:, blk, 3:4], GMW, ["sst3"], ["stat"])
                S.add("dve", lambda e: e.tensor_tensor(out=gob[:], in0=go[:], in1=goB[:],
                                                       op=ALU.mult),
                      reads=["go", "goB"], writes=["gob"])
                bank = tb.next()
                gb = blk % 2

                def trg(e, bank=bank):
                    pst = ps[:, bank, :].bitcast(BF16)
                    ins = None
                    for c in range(8):
                        ins = e.transpose(out=pst[:, c * 128:(c + 1) * 128],
                                          in_=gob[:, c * 128:(c + 1) * 128], identity=cx.ident[:])
                    return ins
                S.add("pe", trg, reads=["gob", "ident"], writes=[f"ps{bank}"])
                S.add("act", lambda e, bank=bank, gb=gb: e.copy(
                    out=goT[gb][:], in_=ps[:, bank, :].bitcast(BF16).rearrange(
                        "p (c t) -> p c t", c=8)), reads=[f"ps{bank}"], writes=[f"goT{gb}"])
                S.add("sp", lambda e, gb=gb, blk=blk: e.dma_start(
                    out=sc["mixT"][8:16].rearrange("c p n -> p c n")[:, :, blk * 128:(blk + 1) * 128],
                    in_=goT[gb][:]), reads=[f"goT{gb}"], dma=f"goTs{gb}")

            a_mm(0)
            a_ev(0)
            for s in range(4):
                if s + 1 < 4:
                    a_mm(s + 1)
                if s == 2 and t + 1 < NT:
                    phase0(t + 1)
                b_chain(s)
                if s + 1 < 4:
                    a_ev(s + 1)
        S.add("sp", lambda e: e.dma_start(out=sc["stat"], in_=stat[:]), reads=["stat"], dma="stats")
        S.add("sp", lambda e: e.dma_start(out=sc["krr"], in_=krr[:]), reads=["krr"], dma="krrs")
        S.emit()


def stage_qk(nc, cx, W, sc, NTOK):
    NB = NTOK // 128
    SCALE = 192.0 ** -0.5
    with ExitStack() as es:
        S = Sched(nc, es)
        sb = lambda name, shape, dt: es.enter_context(nc.sbuf_tensor(f"{name}_{S.uid}", shape, dt))
        wuq = sb("wuq", [128, 4, 1536], BF16)
        wk = sb("wk", [128, 2, 8, 128], BF16)
        gq = sb("gq", [128, 192], F32)
        gk = sb("gk", [128, 192], F32)
        stat = sb("stat", [128, NB, 8], F32)
        krr = sb("krr", [128, NB, 64], F32)
        cs = sb("cs", [128, NB, 32], F32)
        sn = sb("sn", [128, NB, 32], F32)
        ksc = sb("ksc", [128, NB, 8], F32)
        cq = [sb(f"cq{i}", [128, 4, 512], BF16) for i in range(2)]
        ckv = [sb(f"ckv{i}", [128, 2, 512], BF16) for i in range(2)]
        sq = sb("sq", [128, 1536], F32)
        ssq = sb("ssq", [128, 8], F32)
        ssk = sb("ssk", [128, 8], F32)
        rt = sb("rt", [128, 8], F32)
        r2 = sb("r2", [128, 1], F32)
        qn = sb("qn", [128, 8, 192], F32)
        qb = sb("qb", [128, 8, 192], BF16)
        kb = sb("kb", [128, 8, 128], BF16)
        krn = sb("krn", [128, 64], F32)
        krb = sb("krb", [128, 64], BF16)
        ra = sb("ra", [128, 8, 32], F32)
        rb = sb("rb", [128, 8, 32], F32)
        qTn = [sb(f"qTn{i}", [128, 8, 128], BF16) for i in range(2)]
        qTr = [sb(f"qTr{i}", [64, 8, 128], BF16) for i in range(2)]
        kTn = [sb(f"kTn{i}", [128, 8, 128], BF16) for i in range(2)]
        kTr = [sb(f"kTr{i}", [64, 128], BF16) for i in range(2)]
        ps = es.enter_context(nc.psum_tensor(f"ps_{S.uid}", [128, 8, 512], F32))

        for (tl, src, nm) in ((stat, sc["stat"], "stat"), (krr, sc["krr"], "krr"),
                              (cs, sc["cos"], "cs"), (sn, sc["sin"], "sn")):
            S.add("sp", lambda e, tl=tl, src=src: e.dma_start(out=tl[:], in_=src),
                  writes=[nm], dma=nm)
        S.add("sp", lambda e: e.dma_start(out=gq[:], in_=bcast_row(W["q_norm"], 128)),
              writes=["gq"], dma="gq")
        S.add("sp", lambda e: e.dma_start(out=gk[:], in_=bcast_row(W["k_norm"], 128)),
              writes=["gk"], dma="gk")
        load_w_resident(S, wuq, W["w_uq"], 4, 1536, "wuq", chunk=1536)
        wkv = W["w_ukv"].rearrange("(kc p) (h j) -> p kc h j", p=128, j=256)
        for kc in range(2):
            S.add("pool", lambda e, kc=kc: e.dma_start(out=wk[:, kc], in_=wkv[:, kc, :, 0:128]),
                  writes=["wk"], dma=f"wk{kc}")

        cqv = sc["cqT"].rearrange("c p n -> p c n")
        ckvv = sc["ckvT"].rearrange("c p n -> p c n")
        for blk in range(NB):
            t, s = divmod(blk, 4)
            cbuf = t % 2
            if s == 0:
                S.add("sp", lambda e, t=t, cbuf=cbuf: e.dma_start(
                    out=cq[cbuf][:], in_=cqv[:, :, t * 512:(t + 1) * 512]),
                    writes=[f"cq{cbuf}"], dma=f"cq{cbuf}")
                S.add("sp", lambda e, t=t, cbuf=cbuf: e.dma_start(
                    out=ckv[cbuf][:], in_=ckvv[:, :, t * 512:(t + 1) * 512]),
                    writes=[f"ckv{cbuf}"], dma=f"ckv{cbuf}")
            tok = slice(s * 128, (s + 1) * 128)

            def mq(e, cbuf=cbuf, tok=tok):
                ins = None
                for g in range(3):
                    for kc in range(4):
                        ins = e.matmul(out=ps[:, g, :], lhsT=cq[cbuf][:, kc, tok],
                                       rhs=wuq[:, kc, g * 512:(g + 1) * 512],
                                       start=(kc == 0), stop=(kc == 3))
                return ins
            S.add("pe", mq, reads=[f"cq{cbuf}", "wuq0"], writes=["ps0", "ps1", "ps2"])

            def mk(e, cbuf=cbuf, tok=tok):
                ins = None
                for g in range(2):
                    for kc in range(2):
                        ins = e.matmul(out=ps[:, 3 + g, :], lhsT=ckv[cbuf][:, kc, tok],
                                       rhs=wk[:, kc, g * 4:(g + 1) * 4, :].rearrange("p h j -> p (h j)"),
                                       start=(kc == 0), stop=(kc == 1))
                return ins
            S.add("pe", mk, reads=[f"ckv{cbuf}", "wk"], writes=["ps3", "ps4"])
            qps = ps[:, 0:3, :]
            qv = ps[:, 0:3, :].rearrange("p a b -> p (a b)").rearrange("p (h d) -> p h d", h=8)
            kps = ps[:, 3:5, :]
            kv_ = ps[:, 3:5, :].rearrange("p a b -> p (a b)").rearrange("p (h d) -> p h d", h=8)
            S.add("act", lambda e, qps=qps: e.activation(
                out=sq[:].rearrange("p (a b) -> p a b", a=3), in_=qps, func=AF.Square),
                reads=["ps0", "ps1", "ps2"], writes=["sq"])
            S.add("dve", lambda e: e.reduce_sum(
                out=ssq[:], in_=sq[:].rearrange("p (h d) -> p h d", h=8), axis=AX.X),
                reads=["sq"], writes=["ssq"])
            S.add("dve", lambda e, blk=blk: e.tensor_tensor(
                out=r2[:], in0=stat[:, blk, 0:1], in1=stat[:, blk, 0:1], op=ALU.mult),
                reads=["stat"], writes=["r2"])
            S.add("dve", lambda e: e.tensor_scalar(
                out=ssq[:], in0=ssq[:], scalar1=r2[:, 0:1], scalar2=None, op0=ALU.mult),
                reads=["ssq", "r2"], writes=["ssq"])
            rstd_op(S, cx, ssq[:], rt[:], 192.0, ["ssq"], ["rt"])
            S.add("dve", lambda e, blk=blk: e.tensor_scalar(
                out=rt[:], in0=rt[:], scalar1=stat[:, blk, 0:1], scalar2=None, op0=ALU.mult),
                reads=["rt", "stat"], writes=["rt"])
            S.add("dve", lambda e, qv=qv: e.tensor_tensor(
                out=qn[:], in0=qv, in1=rt[:].unsqueeze(2).to_broadcast([128, 8, 192]),
                op=ALU.mult), reads=["ps0", "ps1", "ps2", "rt"], writes=["qn"])
            S.add("dve", lambda e: e.tensor_tensor(
                out=qn[:], in0=qn[:], in1=gq[:].unsqueeze(1).to_broadcast([128, 8, 192]),
                op=ALU.mult), reads=["qn", "gq"], writes=["qn"])
            S.add("act", lambda e: e.copy(out=qb[:, :, 0:128], in_=qn[:, :, 0:128]),
                  reads=["qn"], writes=["qbn"])
            csb = cs[:, blk, :].unsqueeze(1).to_broadcast([128, 8, 32])
            snb = sn[:, blk, :].unsqueeze(1).to_broadcast([128, 8, 32])
            x1 = qn[:, :, 128:160]
            x2 = qn[:, :, 160:192]
            S.add("dve", lambda e, x1=x1, csb=csb: e.tensor_tensor(out=ra[:], in0=x1, in1=csb,
                                                                   op=ALU.mult),
                  reads=["qn", "cs"], writes=["ra"])
            S.add("dve", lambda e, x2=x2, snb=snb: e.tensor_tensor(out=rb[:], in0=x2, in1=snb,
                                                                   op=ALU.mult),
                  reads=["qn", "sn"], writes=["rb"])
            S.add("dve", lambda e: e.tensor_tensor(out=qb[:, :, 128:160], in0=ra[:], in1=rb[:],
                                                   op=ALU.subtract),
                  reads=["ra", "rb"], writes=["qbr1"])
            S.add("dve", lambda e, x2=x2, csb=csb: e.tensor_tensor(out=ra[:], in0=x2, in1=csb,
                                                                   op=ALU.mult),
                  reads=["qn", "cs"], writes=["ra"])
            S.add("dve", lambda e, x1=x1, snb=snb: e.tensor_tensor(out=rb[:], in0=x1, in1=snb,
                                                                   op=ALU.mult),
                  reads=["qn", "sn"], writes=["rb"])
            S.add("dve", lambda e: e.tensor_tensor(out=qb[:, :, 160:192], in0=ra[:], in1=rb[:],
                                                   op=ALU.add),
                  reads=["ra", "rb"], writes=["qbr2"])
            S.add("act", lambda e, kps=kps: e.activation(
                out=sq[:, 0:1024].rearrange("p (a b) -> p a b", a=2), in_=kps, func=AF.Square),
                reads=["ps3", "ps4"], writes=["sq"])
            S.add("dve", lambda e: e.reduce_sum(
                out=ssk[:], in_=sq[:, 0:1024].rearrange("p (h d) -> p h d", h=8), axis=AX.X),
                reads=["sq"], writes=["ssk"])
            S.add("dve", lambda e, blk=blk: e.tensor_tensor(
                out=r2[:], in0=stat[:, blk, 1:2], in1=stat[:, blk, 1:2], op=ALU.mult),
                reads=["stat"], writes=["r2"])
            S.add("dve", lambda e, blk=blk: e.tensor_scalar(
                out=ssk[:], in0=ssk[:], scalar1=r2[:, 0:1], scalar2=stat[:, blk, 2:3],
                op0=ALU.mult, op1=ALU.add), reads=["ssk", "r2", "stat"], writes=["ssk"])
            rstd_op(S, cx, ssk[:], ssk[:], 192.0, ["ssk"], ["ssk"])
            S.add("dve", lambda e, blk=blk: e.tensor_scalar(
                out=ksc[:, blk, :], in0=ssk[:], scalar1=stat[:, blk, 1:2], scalar2=SCALE,
                op0=ALU.mult, op1=ALU.mult), reads=["ssk", "stat"], writes=["ksc"])
            S.add("dve", lambda e, kv_=kv_: e.tensor_tensor(
                out=kb[:], in0=kv_, in1=gk[:, 0:128].unsqueeze(1).to_broadcast([128, 8, 128]),
                op=ALU.mult), reads=["ps3", "ps4", "gk"], writes=["kb"])
            S.add("dve", lambda e, blk=blk: e.tensor_tensor(
                out=krn[:], in0=krr[:, blk, :], in1=gk[:, 128:192], op=ALU.mult),
                reads=["krr", "gk"], writes=["krn"])
            S.add("dve", lambda e, blk=blk: e.reciprocal(out=r2[:], in_=stat[:, blk, 1:2]),
                  reads=["stat"], writes=["r2"])
            S.add("dve", lambda e, blk=blk: e.tensor_scalar(
                out=krn[:], in0=krn[:], scalar1=r2[:, 0:1], scalar2=None, op0=ALU.mult),
                reads=["krn", "r2"], writes=["krn"])
            c1 = cs[:, blk, :]
            s1 = sn[:, blk, :]
            S.add("dve", lambda e, c1=c1: e.tensor_tensor(out=ra[:, 0, :], in0=krn[:, 0:32], in1=c1,
                                                          op=ALU.mult),
                  reads=["krn", "cs"], writes=["ra"])
            S.add("dve", lambda e, s1=s1: e.tensor_tensor(out=rb[:, 0, :], in0=krn[:, 32:64], in1=s1,
                                                          op=ALU.mult),
                  reads=["krn", "sn"], writes=["rb"])
            S.add("dve", lambda e: e.tensor_tensor(out=krb[:, 0:32], in0=ra[:, 0, :], in1=rb[:, 0, :],
                                                   op=ALU.subtract),
                  reads=["ra", "rb"], writes=["krb1"])
            S.add("dve", lambda e, c1=c1: e.tensor_tensor(out=ra[:, 0, :], in0=krn[:, 32:64], in1=c1,
                                                          op=ALU.mult),
                  reads=["krn", "cs"], writes=["ra"])
            S.add("dve", lambda e, s1=s1: e.tensor_tensor(out=rb[:, 0, :], in0=krn[:, 0:32], in1=s1,
                                                          op=ALU.mult),
                  reads=["krn", "sn"], writes=["rb"])
            S.add("dve", lambda e: e.tensor_tensor(out=krb[:, 32:64], in0=ra[:, 0, :], in1=rb[:, 0, :],
                                                   op=ALU.add),
                  reads=["ra", "rb"], writes=["krb2"])
            ob = blk % 2

            def trq(e):
                p5 = ps[:, 5, :].bitcast(BF16)
                p6 = ps[:, 6, :].bitcast(BF16)
                p7 = ps[:, 7, :].bitcast(BF16)
                ins = None
                for hh in range(8):
                    ins = e.transpose(out=p5[:, hh * 128:(hh + 1) * 128], in_=qb[:, hh, 0:128],
                                      identity=cx.ident[:])
                for hh in range(8):
                    ins = e.transpose(out=p6[0:64, hh * 128:(hh + 1) * 128], in_=qb[:, hh, 128:192],
                                      identity=cx.ident[:])
                for hh in range(8):
                    ins = e.transpose(out=p7[:, hh * 128:(hh + 1) * 128], in_=kb[:, hh, :],
                                      identity=cx.ident[:])
                return ins
            S.add("pe", trq, reads=["qbn", "qbr1", "qbr2", "kb", "ident"],
                  writes=["ps5", "ps6", "ps7"])
            S.add("act", lambda e, ob=ob: e.copy(
                out=qTn[ob][:], in_=ps[:, 5, :].bitcast(BF16).rearrange("p (h t) -> p h t", h=8)),
                reads=["ps5"], writes=[f"qTn{ob}"])
            S.add("dve", lambda e, ob=ob: e.tensor_copy(
                out=qTr[ob][:], in_=ps[0:64, 6, :].bitcast(BF16).rearrange("p (h t) -> p h t", h=8)),
                reads=["ps6"], writes=[f"qTr{ob}"])
            S.add("act", lambda e, ob=ob: e.copy(
                out=kTn[ob][:], in_=ps[:, 7, :].bitcast(BF16).rearrange("p (h t) -> p h t", h=8)),
                reads=["ps7"], writes=[f"kTn{ob}"])

            def trk(e):
                p6 = ps[:, 6, :].bitcast(BF16)
                return e.transpose(out=p6[0:64, 0:128], in_=krb[:, :], identity=cx.ident[:])
            S.add("pe", trk, reads=["krb1", "krb2", "ident"], writes=["ps6"])
            S.add("dve", lambda e, ob=ob: e.tensor_copy(
                out=kTr[ob][:], in_=ps[0:64, 6, :].bitcast(BF16)[:, 0:128]),
                reads=["ps6"], writes=[f"kTr{ob}"])
            tsl = slice(blk * 128, (blk + 1) * 128)
            S.add("sp", lambda e, ob=ob, tsl=tsl: e.dma_start(
                out=sc["qTn"].rearrange("h p n -> p h n")[:, :, tsl], in_=qTn[ob][:]),
                reads=[f"qTn{ob}"], dma=f"qTns{ob}")
            S.add("sp", lambda e, ob=ob, tsl=tsl: e.dma_start(
                out=sc["qTr"].rearrange("h p n -> p h n")[:, :, tsl], in_=qTr[ob][:]),
                reads=[f"qTr{ob}"], dma=f"qTrs{ob}")
            S.add("sp", lambda e, ob=ob, tsl=tsl: e.dma_start(
                out=sc["kTn"].rearrange("h p n -> p h n")[:, :, tsl], in_=kTn[ob][:]),
                reads=[f"kTn{ob}"], dma=f"kTns{ob}")
            S.add("sp", lambda e, ob=ob, tsl=tsl: e.dma_start(
                out=sc["kTr"][:, tsl], in_=kTr[ob][:]),
                reads=[f"kTr{ob}"], dma=f"kTrs{ob}")
        S.add("sp", lambda e: e.dma_start(out=sc["ksc"], in_=ksc[:]), reads=["ksc"], dma="kscs")
        S.emit()


def stage_attn(nc, cx, W, sc, NTOK):
    NB = NTOK // 128
    NQT = NTOK // 512
    with ExitStack() as es:
        S = Sched(nc, es)
        sb = lambda name, shape, dt: es.enter_context(nc.sbuf_tensor(f"{name}_{S.uid}", shape, dt))
        ckv = sb("ckv", [128, 2, NTOK], BF16)
        wv = sb("wv", [128, 2, 8, 128], BF16)
        stat = sb("stat", [128, NB, 8], F32)
        ksc = sb("ksc", [128, NB, 8], F32)
        gaB = sb("gaB", [128, 1024], F32)
        kTr = sb("kTr", [64, NTOK], BF16)
        qTn = [sb(f"qTn{i}", [128, NTOK], BF16) for i in range(2)]
        qTr = [sb(f"qTr{i}", [64, NTOK], BF16) for i in range(2)]
        kTn = [sb(f"kTn{i}", [128, NTOK], BF16) for i in range(2)]
        vaug = [sb(f"vaug{i}", [128, NB, 132], BF16) for i in range(2)]
        aT = [sb(f"aT{i}", [128, NTOK], BF16) for i in range(2)]
        pT = [sb(f"pT{i}", [128, 512], BF16) for i in range(4)]
        tri = sb("tri", [128, 128], BF16)
        trif = sb("trif", [128, 128], F32)
        ssa = sb("ssa", [128, NB, 8], F32)
        rden = [sb(f"rden{i}", [128, 1], F32) for i in range(2)]
        af = [sb(f"af{i}", [128, 128], F32) for i in range(2)]
        ab = [sb(f"ab{i}", [128, 128], BF16) for i in range(2)]
        junk = sb("junk", [128, 128], BF16)
        ps = es.enter_context(nc.psum_tensor(f"ps_{S.uid}", [128, 8, 512], F32))

        S.add("sp", lambda e: e.dma_start(out=stat[:], in_=sc["stat"]), writes=["stat"], dma="stat")
        S.add("sp", lambda e: e.dma_start(out=ksc[:], in_=sc["ksc"]), writes=["ksc"], dma="ksc")
        S.add("sp", lambda e: e.dma_start(out=gaB[:], in_=bcast_row(W["attn_out_norm"], 128)),
              writes=["gaB"], dma="gaB")
        S.add("sp", lambda e: e.dma_start(out=kTr[:], in_=sc["kTr"]), writes=["kTr"], dma="kTr")
        S.add("sp", lambda e: e.dma_start(out=ckv[:], in_=sc["ckvT"].rearrange("c p n -> p c n")),
              writes=["ckv"], dma="ckv")
        wkv = W["w_ukv"].rearrange("(kc p) (h j) -> p kc h j", p=128, j=256)
        for kc in range(2):
            S.add("pool", lambda e, kc=kc: e.dma_start(out=wv[:, kc], in_=wkv[:, kc, :, 128:256]),
                  writes=["wv"], dma=f"wv{kc}")
        def mktri(e):
            e.memset(trif[:], 1.0)
            return e.affine_select(out=trif[:], in_=trif[:], pattern=[[1, 128]],
                                   compare_op=ALU.is_ge, fill=0.0, base=0, channel_multiplier=-1)
        S.add("pool", mktri, writes=["trif"])
        S.add("dve", lambda e: e.tensor_copy(out=tri[:], in_=trif[:]), reads=["trif"],
              writes=["tri"])
        for i in range(2):
            S.add("dve", lambda e, i=i: e.memset(vaug[i][:, :, 128:132], 1.0),
                  writes=[f"vaug{i}"])
        sbank = Rot([0, 1, 2, 3])
        pcnt = 0
        ecnt = 0
        for h in range(NH):
            hb_ = h % 2
            S.add("sp", lambda e, h=h, hb_=hb_: e.dma_start(out=qTn[hb_][:], in_=sc["qTn"][h]),
                  writes=[f"qTn{hb_}"], dma=f"qTn{hb_}")
            S.add("sp", lambda e, h=h, hb_=hb_: e.dma_start(out=qTr[hb_][:], in_=sc["qTr"][h]),
                  writes=[f"qTr{hb_}"], dma=f"qTr{hb_}")
            S.add("sp", lambda e, h=h, hb_=hb_: e.dma_start(out=kTn[hb_][:], in_=sc["kTn"][h]),
                  writes=[f"kTn{hb_}"], dma=f"kTn{hb_}")
            for blk in range(NB):
                bank = sbank.next()

                def mv(e, blk=blk, bank=bank, h=h):
                    ins = None
                    for kc in range(2):
                        ins = e.matmul(out=ps[:, bank, 0:128],
                                       lhsT=ckv[:, kc, blk * 128:(blk + 1) * 128],
                                       rhs=wv[:, kc, h, :], start=(kc == 0), stop=(kc == 1))
                    return ins
                S.add("pe", mv, reads=["ckv", "wv"], writes=[f"ps{bank}"])
                S.add("act", lambda e, blk=blk, bank=bank, hb_=hb_: e.activation(
                    out=vaug[hb_][:, blk, 0:128], in_=ps[:, bank, 0:128], func=AF.Copy,
                    scale=stat[:, blk, 1:2]), reads=[f"ps{bank}", "stat"], writes=[f"vaug{hb_}"])
            items = [(qt, kbi) for qt in range(NQT) for kbi in range(4 * qt + 4)]
            info = {}
            LOOK = 2

            def emit_s(it, h=h, hb_=hb_):
                nonlocal pcnt
                qt, kbi = it
                q0 = qt * 512
                j = kbi - 4 * qt
                off = 128 * j if j > 0 else 0
                n = 512 - off
                bank = sbank.next()
                ksl = slice(kbi * 128, (kbi + 1) * 128)

                def ms(e, bank=bank, ksl=ksl, off=off, n=n, q0=q0):
                    e.matmul(out=ps[:, bank, 0:n], lhsT=kTn[hb_][:, ksl],
                             rhs=qTn[hb_][:, q0 + off:q0 + 512], start=True, stop=False)
                    return e.matmul(out=ps[:, bank, 0:n], lhsT=kTr[:, ksl],
                                    rhs=qTr[hb_][:, q0 + off:q0 + 512], start=False, stop=True)
                S.add("pe", ms, reads=[f"kTn{hb_}", f"qTn{hb_}", f"qTr{hb_}", "kTr"],
                      writes=[f"ps{bank}"])
                pb = pcnt % 4
                pcnt += 1
                S.add("act", lambda e, bank=bank, n=n, pb=pb, kbi=kbi: e.activation(
                    out=pT[pb][:, 0:n], in_=ps[:, bank, 0:n], func=AF.Exp,
                    scale=ksc[:, kbi, h:h + 1]), reads=[f"ps{bank}", "ksc"], writes=[f"pT{pb}"])
                if j >= 0:
                    S.add("dve", lambda e, pb=pb: e.tensor_tensor(
                        out=pT[pb][:, 0:128], in0=pT[pb][:, 0:128], in1=tri[:], op=ALU.mult),
                        reads=[f"pT{pb}", "tri"], writes=[f"pT{pb}"])
                info[it] = (pb, off, j)

            def emit_pv(it, h=h, hb_=hb_):
                nonlocal ecnt
                qt, kbi = it
                pb, off, j = info.pop(it)
                qs0 = j if j > 0 else 0

                def mpv(e, pb=pb, kbi=kbi, qs0=qs0, off=off, qt=qt):
                    ins = None
                    for qs in range(qs0, 4):
                        c0 = qs * 128 - off
                        ins = e.matmul(out=ps[:, 4 + qs, 0:129], lhsT=pT[pb][:, c0:c0 + 128],
                                       rhs=vaug[hb_][:, kbi, 0:129], start=(kbi == 0),
                                       stop=(kbi == 4 * qt + qs))
                    return ins
                S.add("pe", mpv, reads=[f"pT{pb}", f"vaug{hb_}"],
                      writes=[f"ps{4 + qs}" for qs in range(qs0, 4)])
                if kbi != 4 * qt + 3:
                    return
                for qs in range(4):
                    blk = qt * 4 + qs
                    eb = ecnt % 2
                    ecnt += 1
                    acc = ps[:, 4 + qs, :]
                    S.add("dve", lambda e, acc=acc, eb=eb: e.reciprocal(out=rden[eb][:],
                                                                        in_=acc[:, 128:129]),
                          reads=[f"ps{4 + qs}"], writes=[f"rden{eb}"])
                    S.add("act", lambda e, acc=acc, eb=eb: e.activation(
                        out=af[eb][:], in_=acc[:, 0:128], func=AF.Copy, scale=rden[eb][:, 0:1]),
                        reads=[f"ps{4 + qs}", f"rden{eb}"], writes=[f"af{eb}"])
                    S.add("act", lambda e, eb=eb, blk=blk: e.activation(
                        out=junk[:], in_=af[eb][:], func=AF.Square, accum_out=ssa[:, blk, h:h + 1]),
                        reads=[f"af{eb}"], writes=["junk", "ssa"])
                    S.add("dve", lambda e, eb=eb: e.tensor_tensor(
                        out=ab[eb][:], in0=af[eb][:], in1=gaB[:, h * 128:(h + 1) * 128],
                        op=ALU.mult), reads=[f"af{eb}", "gaB"], writes=[f"ab{eb}"])
                    bank = sbank.next()
                    S.add("pe", lambda e, eb=eb, bank=bank: e.transpose(
                        out=ps[:, bank, :].bitcast(BF16)[:, 0:128], in_=ab[eb][:],
                        identity=cx.ident[:]), reads=[f"ab{eb}", "ident"], writes=[f"ps{bank}"])
                    S.add("dve", lambda e, bank=bank, blk=blk: e.tensor_copy(
                        out=aT[hb_][:, blk * 128:(blk + 1) * 128],
                        in_=ps[:, bank, :].bitcast(BF16)[:, 0:128]),
                        reads=[f"ps{bank}"], writes=[f"aT{hb_}"])

            for i in range(len(items) + LOOK):
                if i < len(items):
                    emit_s(items[i])
                if i >= LOOK:
                    emit_pv(items[i - LOOK])
            S.add("sp", lambda e, h=h, hb_=hb_: e.dma_start(out=sc["mixT"][h], in_=aT[hb_][:]),
                  reads=[f"aT{hb_}"], dma=f"aTs{hb_}")
        S.add("dve", lambda e: e.reduce_sum(out=stat[:, :, 4:5], in_=ssa[:], axis=AX.X),
              reads=["ssa", "stat"], writes=["stat"])
        rstd_op(S, cx, stat[:, :, 4:5], stat[:, :, 4:5], 1024.0, ["stat"], ["stat"])
        S.add("sp", lambda e: e.dma_start(out=sc["stat"], in_=stat[:]), reads=["stat"], dma="stats")
        S.emit()


def stage_wout(nc, cx, h, W, sc, NTOK):
    NB = NTOK // 128
    T = 512
    NT = NTOK // T
    with ExitStack() as es:
        S = Sched(nc, es)
        sb = lambda name, shape, dt: es.enter_context(nc.sbuf_tensor(f"{name}_{S.uid}", shape, dt))
        wo = sb("wo", [128, 16, D], BF16)
        stat = sb("stat", [128, NB, 8], F32)
        mx = [sb(f"mx{i}", [128, 16, T], BF16) for i in range(2)]
        hb = [sb(f"hb{i}", [128, 512], F32) for i in range(4)]
        ps = es.enter_context(nc.psum_tensor(f"ps_{S.uid}", [128, 8, 512], F32))
        S.add("sp", lambda e: e.dma_start(out=stat[:], in_=sc["stat"]), writes=["stat"], dma="stat")
        load_w_resident(S, wo, W["w_out"], 16, D, "wo", chunk=512)
        h_t = h.rearrange("(n p) d -> n p d", p=128)
        mv = sc["mixT"].rearrange("c p n -> p c n")
        pb = Rot([0, 1, 2, 3])
        hcnt = 0
        for t in range(NT):
            mb = t % 2
            for half in range(2):
                S.add("sp", lambda e, t=t, mb=mb, half=half: e.dma_start(
                    out=mx[mb][:, half * 8:(half + 1) * 8, :],
                    in_=mv[:, half * 8:(half + 1) * 8, t * T:(t + 1) * T]),
                    writes=[f"mx{mb}"], dma=f"mx{mb}")
            its = [(s, g) for s in range(4) for g in range(4)]

            def ld(k, t=t):
                s_, g_ = its[k]
                blk_ = t * 4 + s_
                hbi_ = k % 4
                S.add("sp", lambda e: e.dma_start(
                    out=hb[hbi_][:], in_=h_t[blk_][:, g_ * 512:(g_ + 1) * 512]),
                    writes=[f"hb{hbi_}"], dma=f"hbl{hbi_}")
            ld(0)
            ld(1)
            for k, (s, g) in enumerate(its):
                blk = t * 4 + s
                pa = pb.next()
                bank_a = pa
                bank_g = pa + 4

                def mm(e, mb=mb, s=s, g=g, bank_a=bank_a, bank_g=bank_g):
                    ins = None
                    for c in range(16):
                        bk = bank_a if c < 8 else bank_g
                        ins = e.matmul(out=ps[:, bk, :], lhsT=mx[mb][:, c, s * 128:(s + 1) * 128],
                                       rhs=wo[:, c, g * 512:(g + 1) * 512],
                                       start=(c % 8 == 0), stop=(c % 8 == 7))
                    return ins
                S.add("pe", mm, reads=[f"mx{mb}", f"wo{g}"],
                      writes=[f"ps{bank_a}", f"ps{bank_g}"])
                hbi = k % 4
                S.add("dve", lambda e, hbi=hbi, bank_a=bank_a, blk=blk: e.scalar_tensor_tensor(
                    out=hb[hbi][:], in0=ps[:, bank_a, :], scalar=stat[:, blk, 4:5],
                    in1=hb[hbi][:], op0=ALU.mult, op1=ALU.add),
                    reads=[f"ps{bank_a}", f"hb{hbi}", "stat"], writes=[f"hb{hbi}"])
                S.add("dve", lambda e, hbi=hbi, bank_g=bank_g, blk=blk: e.scalar_tensor_tensor(
                    out=hb[hbi][:], in0=ps[:, bank_g, :], scalar=stat[:, blk, 3:4],
                    in1=hb[hbi][:], op0=ALU.mult, op1=ALU.add),
                    reads=[f"ps{bank_g}", f"hb{hbi}", "stat"], writes=[f"hb{hbi}"])
                if k + 2 < len(its):
                    ld(k + 2)
                S.add("sp", lambda e, hbi=hbi, blk=blk, g=g: e.dma_start(
                    out=h_t[blk][:, g * 512:(g + 1) * 512], in_=hb[hbi][:]),
                    reads=[f"hb{hbi}"], dma=f"hbs{hbi}")
        S.emit()


def stage_ple(nc, cx, h, p_in, W, NTOK):
    NB = NTOK // 128
    with ExitStack() as es:
        S = Sched(nc, es)
        sb = lambda name, shape, dt: es.enter_context(nc.sbuf_tensor(f"{name}_{S.uid}", shape, dt))
        wg = sb("wg", [128, 16, D], BF16)
        wp = sb("wp", [128, 2, D], BF16)
        gB = sb("gB", [128, D], F32)
        peB = sb("peB", [128, D], F32)
        xin = [sb(f"xin{i}", [128, D], F32) for i in range(2)]
        xn = sb("xn", [128, D], BF16)
        ss = sb("ss", [128, 1], F32)
        rstd = sb("rstd", [128, 1], F32)
        xnT = [sb(f"xnT{i}", [128, 16, 128], BF16) for i in range(2)]
        pin = [sb(f"pin{i}", [128, PLE], F32) for i in range(2)]
        pbf = sb("pbf", [128, PLE], BF16)
        pT = [sb(f"pT{i}", [128, 2, 128], BF16) for i in range(2)]
        sse = sb("sse", [128, 4], F32)
        re_ = sb("re", [128, 1], F32)
        junk = sb("junk", [128, 512], BF16)
        sg = [sb(f"sg{i}", [128, 512], F32) for i in range(2)]
        t1 = [sb(f"t1{i}", [128, 512], F32) for i in range(2)]
        ps = es.enter_context(nc.psum_tensor(f"ps_{S.uid}", [128, 8, 512], F32))
        S.add("sp", lambda e: e.dma_start(out=gB[:], in_=bcast_row(W["ple_gate_norm"], 128)),
              writes=["gB"], dma="gB")
        S.add("sp", lambda e: e.dma_start(out=peB[:], in_=bcast_row(W["ple_norm"], 128)),
              writes=["peB"], dma="peB")
        load_w_resident(S, wp, W["w_ple"], 2, D, "wp", chunk=D)
        load_w_resident(S, wg, W["w_ple_gate"], 16, D, "wg", chunk=512)
        h_t = h.rearrange("(n p) d -> n p d", p=128)
        p_t = p_in.rearrange("(n p) d -> n p d", p=128)
        tb = Rot([0, 1, 2, 3])
        cnt = 0
        def ple_loads(blk):
            b = blk % 2
            S.add("sp", lambda e: e.dma_start(out=xin[b][:], in_=h_t[blk]),
                  writes=[f"xin{b}"], dma=f"xin{b}")
            S.add("sp", lambda e: e.dma_start(out=pin[b][:], in_=p_t[blk]),
                  writes=[f"pin{b}"], dma=f"pin{b}")
        def ple_nt(blk):
            b = blk % 2
            norm_transpose(S, cx, h_t[blk], xin[b][:], xn[:], ss[:], rstd[:], gB[:], xnT[b][:],
                           ps, tb, b, 16, f"xnT{b}", load=False)
        ple_loads(0)
        ple_nt(0)
        for blk in range(NB):
            b = blk % 2
            if blk + 1 < NB:
                ple_loads(blk + 1)
            S.add("dve", lambda e, b=b: e.tensor_copy(out=pbf[:], in_=pin[b][:]),
                  reads=[f"pin{b}"], writes=["pbf"])
            bank = tb.next()

            def trp(e, bank=bank):
                pst = ps[:, bank, :].bitcast(BF16)
                ins = None
                for c in range(2):
                    ins = e.transpose(out=pst[:, c * 128:(c + 1) * 128],
                                      in_=pbf[:, c * 128:(c + 1) * 128], identity=cx.ident[:])
                return ins
            S.add("pe", trp, reads=["pbf", "ident"], writes=[f"ps{bank}"])
            S.add("dve", lambda e, bank=bank, b=b: e.tensor_copy(
                out=pT[b][:], in_=ps[:, bank, :].bitcast(BF16)[:, 0:256].rearrange(
                    "p (c t) -> p c t", c=2)), reads=[f"ps{bank}"], writes=[f"pT{b}"])

            def me(e, b=b):
                ins = None
                for g in range(4):
                    for c in range(2):
                        ins = e.matmul(out=ps[:, 4 + g, :], lhsT=pT[b][:, c, :],
                                       rhs=wp[:, c, g * 512:(g + 1) * 512], start=(c == 0),
                                       stop=(c == 1))
                return ins
            S.add("pe", me, reads=[f"pT{b}", "wp0"], writes=["ps4", "ps5", "ps6", "ps7"])
            for g in range(4):
                S.add("act", lambda e, g=g: e.activation(out=junk[:], in_=ps[:, 4 + g, :],
                                                         func=AF.Square, accum_out=sse[:, g:g + 1]),
                      reads=[f"ps{4 + g}"], writes=["junk", "sse"])
            S.add("dve", lambda e: e.reduce_sum(out=re_[:], in_=sse[:], axis=AX.X),
                  reads=["sse"], writes=["re"])
            rstd_op(S, cx, re_[:], re_[:], float(D), ["re"], ["re"])
            if blk + 1 < NB:
                ple_nt(blk + 1)
            for g in range(4):
                bank = tb.next()

                def mg(e, b=b, g=g, bank=bank):
                    ins = None
                    for kc in range(16):
                        ins = e.matmul(out=ps[:, bank, :], lhsT=xnT[b][:, kc, :],
                                       rhs=wg[:, kc, g * 512:(g + 1) * 512], start=(kc == 0),
                                       stop=(kc == 15))
                    return ins
                S.add("pe", mg, reads=[f"xnT{b}", f"wg{g}"], writes=[f"ps{bank}"])
                i2 = cnt % 2
                cnt += 1
                S.add("act", lambda e, bank=bank, i2=i2: e.activation(
                    out=sg[i2][:], in_=ps[:, bank, :], func=AF.Sigmoid),
                    reads=[f"ps{bank}"], writes=[f"sg{i2}"])
                S.add("dve", lambda e, g=g, i2=i2: e.scalar_tensor_tensor(
                    out=t1[i2][:], in0=ps[:, 4 + g, :], scalar=re_[:, 0:1],
                    in1=peB[:, g * 512:(g + 1) * 512], op0=ALU.mult, op1=ALU.mult),
                    reads=[f"ps{4 + g}", "re", "peB"], writes=[f"t1{i2}"])
                S.add("dve", lambda e, i2=i2: e.tensor_tensor(out=t1[i2][:], in0=t1[i2][:],
                                                              in1=sg[i2][:], op=ALU.mult),
                      reads=[f"t1{i2}", f"sg{i2}"], writes=[f"t1{i2}"])
                S.add("pool", lambda e, i2=i2, b=b, g=g: e.tensor_tensor(
                    out=xin[b][:, g * 512:(g + 1) * 512], in0=xin[b][:, g * 512:(g + 1) * 512],
                    in1=t1[i2][:], op=ALU.add), reads=[f"t1{i2}", f"xin{b}"], writes=[f"xin{b}"])
            S.add("sp", lambda e, b=b, blk=blk: e.dma_start(out=h_t[blk], in_=xin[b][:]),
                  reads=[f"xin{b}"], dma=f"xins{b}")
        S.emit()


WNAMES = [("ffn_a_norm", (D,)), ("ffn_a_w1", (D, DFF)), ("ffn_a_w3", (D, DFF)), ("ffn_a_w2", (DFF, D)),
          ("mix_norm", (D,)), ("w_in", (D, INC)), ("q_a_norm", (QL,)), ("w_uq", (QL, NH * 192)),
          ("kv_a_norm", (KVL,)), ("w_ukv", (KVL, NH * 256)), ("q_norm", (192,)), ("k_norm", (192,)),
          ("gm_v_norm", (GMW,)), ("gm_ws", (8, 128, 128)), ("gm_bs", (8, 128)),
          ("attn_out_norm", (1024,)), ("gm_out_norm", (GMW,)), ("w_out", (D, D)),
          ("ffn_b_norm", (D,)), ("ffn_b_w1", (D, DFF)), ("ffn_b_w3", (D, DFF)), ("ffn_b_w2", (DFF, D)),
          ("ple_gate_norm", (D,)), ("w_ple_gate", (D, D)), ("w_ple", (PLE, D)), ("ple_norm", (D,))]

ALL_STAGES = ("ffn_a", "zg", "qk", "attn", "wout", "ffn_b", "ple")


def build_program(NTOK=4096, depth=2, stages=ALL_STAGES, T_FFN=1024, debug=False):
    nc = bass.Bass("TRN2", target_bir_lowering=False)
    NB = NTOK // 128

    def dt(name, shape, dtype=F32, kind="ExternalInput"):
        return nc.dram_tensor(name, list(shape), dtype, kind=kind).ap()
    x = dt("x", [NTOK, D])
    p = dt("p", [depth, NTOK, PLE])
    pos = dt("pos_pm", [128, NB], I32)
    invf = dt("inv_freq", [32])
    Wd = {name: dt(name, (depth,) + shp) for name, shp in WNAMES}
    out = dt("out", [NTOK, D], kind="ExternalOutput")
    sk = "ExternalOutput" if debug else "Internal"
    sc = {
        "cos": dt("sc_cos", [128, NB, 32], F32, sk), "sin": dt("sc_sin", [128, NB, 32], F32, sk),
        "cqT": dt("sc_cqT", [4, 128, NTOK], BF16, sk), "ckvT": dt("sc_ckvT", [2, 128, NTOK], BF16, sk),
        "stat": dt("sc_stat", [128, NB, 8], F32, sk), "krr": dt("sc_krr", [128, NB, 64], F32, sk),
        "qTn": dt("sc_qTn", [8, 128, NTOK], BF16, sk), "qTr": dt("sc_qTr", [8, 64, NTOK], BF16, sk),
        "kTn": dt("sc_kTn", [8, 128, NTOK], BF16, sk), "kTr": dt("sc_kTr", [64, NTOK], BF16, sk),
        "ksc": dt("sc_ksc", [128, NB, 8], F32, sk), "mixT": dt("sc_mixT", [16, 128, NTOK], BF16, sk),
    }
    cx = Ctx()
    with ExitStack() as es:
        POOL["es"] = es
        POOL["sems"] = {}
        load_consts(nc, es, cx)
        if "qk" in stages:
            stage_rope(nc, cx, pos, invf, sc, NTOK)
        first = True
        for i in range(depth):
            W = {k: v[i] for k, v in Wd.items()}
            if "ffn_a" in stages:
                stage_ffn(nc, cx, x if first else out, out, W["ffn_a_norm"], W["ffn_a_w1"],
                          W["ffn_a_w3"], W["ffn_a_w2"], NTOK, T_FFN)
                first = False
            src = x if first else out
            if "zg" in stages:
                stage_zg(nc, cx, src, W, sc, NTOK)
            if "qk" in stages:
                stage_qk(nc, cx, W, sc, NTOK)
            if "attn" in stages:
                stage_attn(nc, cx, W, sc, NTOK)
            if "wout" in stages:
                assert not first
                stage_wout(nc, cx, out, W, sc, NTOK)
            if "ffn_b" in stages:
                stage_ffn(nc, cx, out, out, W["ffn_b_norm"], W["ffn_b_w1"], W["ffn_b_w3"],
                          W["ffn_b_w2"], NTOK, T_FFN)
            if "ple" in stages:
                stage_ple(nc, cx, out, p[i], W, NTOK)
    return nc


INV_FREQ = (10000.0 ** (-np.arange(0, 64, 2, dtype=np.float32) / 64)).astype(np.float32)


def make_in_maps(inputs, ncores, NTOK):
    maps = []
    NB = NTOK // 128
    for c in range(ncores):
        m = {"x": np.ascontiguousarray(inputs["x"][c]),
             "p": np.ascontiguousarray(inputs["p"][:, c]),
             "pos_pm": np.ascontiguousarray(inputs["positions"][c].reshape(NB, 128).T),
             "inv_freq": INV_FREQ}
        for name, _ in WNAMES:
            m[name] = inputs[name]
        maps.append(m)
    return maps


_NC_CACHE = {}


def kernel(**inputs):
    B, NTOK, _ = inputs["x"].shape
    depth = inputs["p"].shape[0]
    key = (NTOK, depth)
    if key not in _NC_CACHE:
        _NC_CACHE[key] = build_program(NTOK=NTOK, depth=depth)
    nc = _NC_CACHE[key]
    inputs = {k: np.asarray(v) for k, v in inputs.items()}
    maps = make_in_maps(inputs, B, NTOK)
    res = run_bass_kernel_spmd(nc, maps, core_ids=list(range(B)))
    return np.stack([np.asarray(r["out"], dtype=np.float32) for r in res.results], axis=0)
```

```python
import numpy as np
from contextlib import ExitStack
import concourse.bass as bass
import concourse.mybir as mybir
from concourse.bass_utils import run_bass_kernel_spmd

F32 = mybir.dt.float32
BF16 = mybir.dt.bfloat16
I32 = mybir.dt.int32
AF = mybir.ActivationFunctionType
ALU = mybir.AluOpType
AX = mybir.AxisListType

D = 2048
DFF = 5504
NH = 8
QL = 512
KVL = 256
ROPE = 64
GMW = 1024
INC = 2880
PLE = 256
EPS = 1e-6


class Lane:
    def __init__(self, name, step):
        self.name = name
        self.step = step
        self.sem = None
        self.total = 0
        self.ops = []


class Op:
    __slots__ = ("eng", "fn", "deps", "lane", "signal", "count", "idx")


ENGS = ("pe", "act", "dve", "pool", "sp")


_UID = [0]
POOL = {"es": None, "sems": {}}


def uid():
    _UID[0] += 1
    return _UID[0]


class Sched:
    def __init__(self, nc, es):
        self.uid = uid()
        self.nc = nc
        self.es = es
        self.ops = {e: [] for e in ENGS}
        self.lanes = {}
        self.res_w = {}
        self.res_r = {}
        self.n = 0
        for e in ENGS:
            self.lanes[e] = Lane(e, 1)

    def lane(self, name):
        if name not in self.lanes:
            self.lanes[name] = Lane(name, 16)
        return self.lanes[name]

    def add(self, eng, fn, reads=(), writes=(), dma=None):
        op = Op()
        op.eng = eng
        op.fn = fn
        op.signal = False
        op.count = None
        op.idx = self.n
        self.n += 1
        op.lane = self.lane(dma) if dma is not None else self.lanes[eng]
        if dma is not None:
            op.signal = True
        deps = {}
        for k in reads:
            w = self.res_w.get(k)
            if w is not None:
                deps[w.idx] = w
        for k in writes:
            w = self.res_w.get(k)
            if w is not None:
                deps[w.idx] = w
            for r in self.res_r.get(k, ()):
                deps[r.idx] = r
        for k in reads:
            self.res_r.setdefault(k, []).append(op)
        for k in writes:
            self.res_w[k] = op
            self.res_r[k] = []
        dl = []
        for d in deps.values():
            if d is op:
                continue
            if d.lane is op.lane and op.eng == "pe" and dma is None:
                continue
            d.signal = True
            dl.append(d)
        op.deps = dl
        op.lane.ops.append(op)
        self.ops[eng].append(op)
        return op

    def emit(self):
        nc = self.nc
        for ln in self.lanes.values():
            if ln.ops:
                ln.ops[-1].signal = True
        lanes = [ln for ln in self.lanes.values() if ln.ops]
        nd = 0
        for ln in lanes:
            if ln.step == 1:
                gname = "e_" + ln.name
            else:
                gname = f"d{nd}"
                nd += 1
            if gname not in POOL["sems"]:
                POOL["sems"][gname] = [POOL["es"].enter_context(nc.semaphore("g_" + gname)), 0]
            ent = POOL["sems"][gname]
            ln.sem = ent[0]
            c = ent[1]
            ln.base = c
            for op in ln.ops:
                if op.signal:
                    c += ln.step
                op.count = c
            ln.total = c
            ent[1] = c
            assert c < 60000, (gname, c)

        def run(eng_obj, ename):
            waited = {}
            for op in self.ops[ename]:
                need = {}
                for d in op.deps:
                    ln = d.lane
                    c = d.count
                    if ln.step == 16:
                        c = max([o.count for o in ln.ops if o.idx < op.idx] + [ln.base])
                    if c <= ln.base:
                        continue
                    if need.get(ln.name, 0) < c:
                        need[ln.name] = c
                for lname, c in need.items():
                    if waited.get(lname, 0) >= c:
                        continue
                    waited[lname] = c
                    eng_obj.wait_ge(self.lanes[lname].sem, c)
                ins = op.fn(eng_obj)
                if op.signal:
                    ins.then_inc(op.lane.sem, op.lane.step)
            for ln in lanes:
                if waited.get(ln.name, 0) < ln.total:
                    eng_obj.wait_ge(ln.sem, ln.total)

        with nc.Block() as block:
            @block.tensor
            def _(e):
                run(e, "pe")

            @block.scalar
            def _(e):
                run(e, "act")

            @block.vector
            def _(e):
                run(e, "dve")

            @block.gpsimd
            def _(e):
                run(e, "pool")

            @block.sync
            def _(e):
                run(e, "sp")


def bcast_row(ap1d, nparts):
    return ap1d.partition_broadcast(nparts)


class Ctx:
    pass


def load_consts(nc, es, cx):
    cx.ident = es.enter_context(nc.sbuf_tensor("ident", [128, 128], BF16))
    cx.identf = es.enter_context(nc.sbuf_tensor("identf", [128, 128], F32))
    cx.ones_bf = es.enter_context(nc.sbuf_tensor("ones_bf", [128, 128], BF16))
    cx.eps_col = es.enter_context(nc.sbuf_tensor("eps_col", [128, 1], F32))
    with ExitStack() as es2:
        S = Sched(nc, es2)

        def mk(e):
            e.memset(cx.identf[:], 0.0)
            return e.affine_select(out=cx.identf[:], in_=cx.identf[:], pattern=[[-1, 128]],
                                   compare_op=ALU.not_equal, fill=1.0, base=0,
                                   channel_multiplier=1)
        S.add("pool", mk, writes=["identf"])
        S.add("dve", lambda e: e.tensor_copy(out=cx.ident[:], in_=cx.identf[:]),
              reads=["identf"], writes=["ident"])
        S.add("dve", lambda e: e.memset(cx.ones_bf[:], 1.0), writes=["ones"])
        S.add("dve", lambda e: e.memset(cx.eps_col[:], EPS), writes=["eps"])
        S.emit()


def norm_tile(S, cx, xin, xn, ss, rstd, gB, junk, Dn, tag, x_key, xn_key):
    def sq(e):
        return e.activation(out=junk, in_=xin, func=AF.Square, accum_out=ss)
    S.add("act", sq, reads=[x_key], writes=[tag + "ss", xn_key])

    def rs(e):
        return e.activation(out=rstd, in_=ss, func=AF.Sqrt, scale=1.0 / Dn, bias=cx.eps_col[:])
    S.add("act", rs, reads=[tag + "ss"], writes=[tag + "rstd"])
    S.add("dve", lambda e: e.reciprocal(out=rstd, in_=rstd), reads=[tag + "rstd"],
          writes=[tag + "rstd"])

    def nm(e):
        return e.scalar_tensor_tensor(out=xn, in0=xin, scalar=rstd, in1=gB,
                                      op0=ALU.mult, op1=ALU.mult)
    S.add("dve", nm, reads=[x_key, tag + "rstd", "gB"], writes=[xn_key])


def stage_ffn(nc, cx, h_src, h_dst, g_norm, w1, w3, w2, NTOK, T, Dm=D, F=DFF):
    KC = Dm // 128
    FC = F // 128
    NS = T // 128
    NHF = T // 512
    NT = NTOK // T
    NG = Dm // 512
    FW = 256 if F % 256 == 0 else 128
    NFW = F // FW
    FPW = FW // 128
    W2G = 4
    with ExitStack() as es:
        S = Sched(nc, es)
        sb = lambda name, shape, dt: es.enter_context(nc.sbuf_tensor(f"{name}_{S.uid}", shape, dt))
        NXB = 2
        xin = [sb(f"xin{i}", [128, Dm], F32) for i in range(NXB)]
        xn = [sb(f"xn{i}", [128, Dm], BF16) for i in range(NXB)]
        ss = [sb(f"ss{i}", [128, 1], F32) for i in range(NXB)]
        rstd = [sb(f"rstd{i}", [128, 1], F32) for i in range(NXB)]
        gB = sb("gB", [128, Dm], F32)
        xnT = sb("xnT", [128, KC, T], BF16)
        hid = sb("hid", [128, FC, T], BF16)
        NWB = 2
        w1b = [sb(f"w1b{i}", [128, KC, FW], BF16) for i in range(NWB)]
        w3b = [sb(f"w3b{i}", [128, KC, FW], BF16) for i in range(NWB)]
        NW2 = 3
        w2b = [sb(f"w2b{i}", [128, W2G, 512], BF16) for i in range(NW2)]
        NHB = 8
        hb = [sb(f"hb{i}", [128, 512], F32) for i in range(NHB)]
        sil = [sb(f"sil{i}", [128, 512], F32) for i in range(2)]
        ps = es.enter_context(nc.psum_tensor(f"ps_{S.uid}", [128, 8, 512], F32))

        S.add("sp", lambda e: e.dma_start(out=gB[:], in_=bcast_row(g_norm, 128)),
              writes=["gB"], dma="gB")

        src_t = h_src.rearrange("(n p) d -> n p d", p=128)
        dst_t = h_dst.rearrange("(n p) d -> n p d", p=128)
        w1v = w1.rearrange("(kc p) f -> p kc f", p=128)
        w3v = w3.rearrange("(kc p) f -> p kc f", p=128)
        w2v = w2.rearrange("(fc p) d -> p fc d", p=128)

        xcnt = 0
        wcnt = 0
        w2cnt = 0
        hcnt = 0
        silc = 0
        tpb = 0
        for t in range(NT):
            def p0x(s, b, t=t):
                row = t * NS + s
                S.add("sp", lambda e: e.dma_start(out=xin[b][:], in_=src_t[row]),
                      writes=[f"xin{b}"], dma=f"xin{b}")
                norm_tile(S, cx, xin[b][:], xn[b][:], ss[b][:], rstd[b][:], gB[:], xn[b][:], Dm,
                          f"n{b}", f"xin{b}", f"xn{b}")

            def p0y(s, b):
                nonlocal tpb
                for half in range(KC // 8):
                    bank = tpb % 8
                    tpb += 1
                    pst = ps[:, bank, :].bitcast(BF16)

                    def tr(e, half=half, pst=pst):
                        ins = None
                        for j in range(8):
                            kc = half * 8 + j
                            ins = e.transpose(out=pst[:, j * 128:(j + 1) * 128],
                                              in_=xn[b][:, kc * 128:(kc + 1) * 128],
                                              identity=cx.ident[:])
                        return ins
                    S.add("pe", tr, reads=[f"xn{b}", "ident"], writes=[f"ps{bank}"])
                    dst = xnT[:, half * 8:(half + 1) * 8, s * 128:(s + 1) * 128]
                    src = pst.rearrange("p (j c) -> p j c", j=8)
                    if half % 2 == 0:
                        S.add("act", lambda e, dst=dst, src=src: e.copy(out=dst, in_=src),
                              reads=[f"ps{bank}"], writes=[f"xnT{s}"])
                    else:
                        S.add("dve", lambda e, dst=dst, src=src: e.tensor_copy(out=dst, in_=src),
                              reads=[f"ps{bank}"], writes=[f"xnT{s}"])
            bsel = []
            for s in range(NS):
                bsel.append(xcnt % NXB)
                xcnt += 1
                p0x(s, bsel[s])
                if s >= 1:
                    p0y(s - 1, bsel[s - 1])
            p0y(NS - 1, bsel[NS - 1])
            for fw in range(NFW):
                wb = wcnt % NWB
                wcnt += 1
                S.add("pool", lambda e, wb=wb, fw=fw: e.dma_start(
                    out=w1b[wb][:], in_=w1v[:, :, fw * FW:(fw + 1) * FW]),
                    writes=[f"w1b{wb}"], dma=f"w1b{wb}")
                S.add("pool", lambda e, wb=wb, fw=fw: e.dma_start(
                    out=w3b[wb][:], in_=w3v[:, :, fw * FW:(fw + 1) * FW]),
                    writes=[f"w3b{wb}"], dma=f"w3b{wb}")
                for fi in range(FPW):
                    fc = fw * FPW + fi
                    par = fc % (8 // (2 * NHF)) if NHF <= 2 else 0
                    base = par * 2 * NHF
                    for mat in range(2):
                        wbuf = (w1b, w3b)[mat][wb]
                        banks = [base + mat * NHF + hf for hf in range(NHF)]

                        def mm(e, wbuf=wbuf, fi=fi, banks=banks):
                            ins = None
                            for kc in range(KC):
                                for hf in range(NHF):
                                    ins = e.matmul(out=ps[:, banks[hf], :],
                                                   lhsT=wbuf[:, kc, fi * 128:(fi + 1) * 128],
                                                   rhs=xnT[:, kc, hf * 512:(hf + 1) * 512],
                                                   start=(kc == 0), stop=(kc == KC - 1))
                            return ins
                        S.add("pe", mm,
                              reads=[("w1b", "w3b")[mat] + str(wb)] + [f"xnT{s}" for s in range(NS)],
                              writes=[f"ps{bk}" for bk in banks])
                    for hf in range(NHF):
                        b1 = base + hf
                        b3 = base + NHF + hf
                        sl = silc % 2
                        silc += 1
                        S.add("act", lambda e, b1=b1, sl=sl: e.activation(
                            out=sil[sl][:], in_=ps[:, b1, :], func=AF.Silu),
                            reads=[f"ps{b1}"], writes=[f"sil{sl}"])
                        S.add("dve", lambda e, b3=b3, sl=sl, fc=fc, hf=hf: e.tensor_tensor(
                            out=hid[:, fc, hf * 512:(hf + 1) * 512], in0=sil[sl][:],
                            in1=ps[:, b3, :], op=ALU.mult),
                            reads=[f"ps{b3}", f"sil{sl}"], writes=[f"hid{fc}"])
            assert NS <= 8
            for g in range(NG):
                for s in range(NS):
                    row = t * NS + s
                    S.add("sp", lambda e, s=s, row=row, g=g: e.dma_start(
                        out=hb[s][:], in_=src_t[row][:, g * 512:(g + 1) * 512]),
                        writes=[f"hb{s}"], dma=f"hbl{s}")
                for f0 in range(0, FC, W2G):
                    nf = min(W2G, FC - f0)
                    wb = w2cnt % NW2
                    w2cnt += 1
                    S.add("pool", lambda e, wb=wb, f0=f0, nf=nf, g=g: e.dma_start(
                        out=w2b[wb][:, 0:nf, :], in_=w2v[:, f0:f0 + nf, g * 512:(g + 1) * 512]),
                        writes=[f"w2b{wb}"], dma=f"w2b{wb}")

                    def mm2(e, wb=wb, f0=f0, nf=nf):
                        ins = None
                        for j in range(nf):
                            fc = f0 + j
                            for s in range(NS):
                                ins = e.matmul(out=ps[:, s, :],
                                               lhsT=hid[:, fc, s * 128:(s + 1) * 128],
                                               rhs=w2b[wb][:, j, :],
                                               start=(fc == 0), stop=(fc == FC - 1))
                        return ins
                    S.add("pe", mm2, reads=[f"w2b{wb}"] + [f"hid{f0 + j}" for j in range(nf)],
                          writes=[f"ps{s}" for s in range(NS)])
                for s in range(NS):
                    row = t * NS + s
                    hbi = s
                    S.add("dve", lambda e, hbi=hbi, s=s: e.scalar_tensor_tensor(
                        out=hb[hbi][:], in0=ps[:, s, :], scalar=0.5, in1=hb[hbi][:],
                        op0=ALU.mult, op1=ALU.add),
                        reads=[f"ps{s}", f"hb{hbi}"], writes=[f"hb{hbi}"])
                    S.add("sp", lambda e, hbi=hbi, row=row, g=g: e.dma_start(
                        out=dst_t[row][:, g * 512:(g + 1) * 512], in_=hb[hbi][:]),
                        reads=[f"hb{hbi}"], dma=f"hbs{hbi}")
        S.emit()


class Rot:
    def __init__(self, items):
        self.items = list(items)
        self.i = 0

    def next(self):
        v = self.items[self.i % len(self.items)]
        self.i += 1
        return v


def rstd_op(S, cx, ss, out, n, rk, wk, mul=None):
    S.add("act", lambda e: e.activation(out=out, in_=ss, func=AF.Sqrt, scale=1.0 / n,
                                        bias=cx.eps_col[:]), reads=rk, writes=wk)
    S.add("dve", lambda e: e.reciprocal(out=out, in_=out), reads=wk, writes=wk)


def norm_transpose(S, cx, src_row, xin, xn, ss, rstd, gB, xnT_dst, ps, banks, b, KC, dst_key, nb=0,
                   load=True):
    if load:
        S.add("sp", lambda e: e.dma_start(out=xin, in_=src_row), writes=[f"xin{b}"], dma=f"xin{b}")
    norm_tile(S, cx, xin, xn, ss, rstd, gB, xn, KC * 128, f"n{nb}", f"xin{b}", f"xn{nb}")
    for half in range((KC + 7) // 8):
        nk = min(8, KC - half * 8)
        bank = banks.next()
        pst = ps[:, bank, :].bitcast(BF16)

        def tr(e, half=half, pst=pst, nk=nk):
            ins = None
            for j in range(nk):
                kc = half * 8 + j
                ins = e.transpose(out=pst[:, j * 128:(j + 1) * 128],
                                  in_=xn[:, kc * 128:(kc + 1) * 128], identity=cx.ident[:])
            return ins
        S.add("pe", tr, reads=[f"xn{nb}", "ident"], writes=[f"ps{bank}"])
        dst = xnT_dst[:, half * 8:half * 8 + nk, :]
        src = pst[:, 0:nk * 128].rearrange("p (j c) -> p j c", j=nk)
        if half % 2 == 0:
            S.add("act", lambda e, dst=dst, src=src: e.copy(out=dst, in_=src),
                  reads=[f"ps{bank}"], writes=[dst_key])
        else:
            S.add("dve", lambda e, dst=dst, src=src: e.tensor_copy(out=dst, in_=src),
                  reads=[f"ps{bank}"], writes=[dst_key])


def load_w_resident(S, wtile, wsrc, KC, ncols, key, chunk=512):
    wv = wsrc.rearrange("(kc p) n -> p kc n", p=128)
    i = 0
    for c0 in range(0, ncols, chunk):
        c1 = min(ncols, c0 + chunk)
        S.add("pool", lambda e, c0=c0, c1=c1: e.dma_start(out=wtile[:, :, c0:c1],
                                                          in_=wv[:, :, c0:c1]),
              writes=[f"{key}{i}"], dma=f"{key}{i}")
        i += 1
    return [f"{key}{j}" for j in range(i)]


def stage_rope(nc, cx, pos_pm, inv_freq, sc, NTOK):
    NB = NTOK // 128
    PI = float(np.pi)
    with ExitStack() as es:
        S = Sched(nc, es)
        sb = lambda name, shape, dt: es.enter_context(nc.sbuf_tensor(f"{name}_{S.uid}", shape, dt))
        posi = sb("posi", [128, NB], I32)
        posf = sb("posf", [128, NB], F32)
        invf = sb("invf", [128, 32], F32)
        ang = sb("ang", [128, NB, 32], F32)
        arg = sb("arg", [128, NB, 32], F32)
        cs = sb("cs", [128, NB, 32], F32)
        sn = sb("sn", [128, NB, 32], F32)
        negpi = sb("negpi", [128, 1], F32)
        S.add("sp", lambda e: e.dma_start(out=posi[:], in_=pos_pm), writes=["posi"], dma="posi")
        S.add("sp", lambda e: e.dma_start(out=invf[:], in_=bcast_row(inv_freq, 128)),
              writes=["invf"], dma="invf")
        S.add("dve", lambda e: e.memset(negpi[:], -PI), writes=["negpi"])
        S.add("dve", lambda e: e.tensor_copy(out=posf[:], in_=posi[:]), reads=["posi"],
              writes=["posf"])
        S.add("dve", lambda e: e.tensor_tensor(
            out=ang[:], in0=posf[:].unsqueeze(2).to_broadcast([128, NB, 32]),
            in1=invf[:].unsqueeze(1).to_broadcast([128, NB, 32]), op=ALU.mult),
            reads=["posf", "invf"], writes=["ang"])
        ki = sb("ki", [128, NB, 32], I32)
        kf = sb("kf", [128, NB, 32], F32)
        C1 = 6.28125
        C2 = 2 * PI - C1
        for (shift, dstt, name) in ((0.0, sn, "sn"), (0.5 * PI, cs, "cs")):
            S.add("dve", lambda e, shift=shift: e.tensor_scalar(
                out=arg[:], in0=ang[:], scalar1=shift, scalar2=None, op0=ALU.add),
                reads=["ang"], writes=["arg"])
            S.add("dve", lambda e: e.tensor_scalar(
                out=kf[:], in0=arg[:], scalar1=1.0 / (2 * PI), scalar2=None, op0=ALU.mult),
                reads=["arg"], writes=["kf"])
            S.add("dve", lambda e: e.tensor_copy(out=ki[:], in_=kf[:]), reads=["kf"], writes=["ki"])
            S.add("dve", lambda e: e.tensor_copy(out=kf[:], in_=ki[:]), reads=["ki"], writes=["kf"])
            S.add("dve", lambda e: e.scalar_tensor_tensor(
                out=arg[:], in0=kf[:], scalar=-C1, in1=arg[:], op0=ALU.mult, op1=ALU.add),
                reads=["kf", "arg"], writes=["arg"])
            S.add("dve", lambda e: e.scalar_tensor_tensor(
                out=arg[:], in0=kf[:], scalar=-C2, in1=arg[:], op0=ALU.mult, op1=ALU.add),
                reads=["kf", "arg"], writes=["arg"])
            S.add("dve", lambda e: e.tensor_scalar(
                out=arg[:], in0=arg[:], scalar1=-3.141592, scalar2=3.141592, op0=ALU.max,
                op1=ALU.min), reads=["arg"], writes=["arg"])
            S.add("act", lambda e, dstt=dstt: e.activation(out=dstt[:], in_=arg[:], func=AF.Sin),
                  reads=["arg"], writes=[name])
        S.add("sp", lambda e: e.dma_start(out=sc["cos"], in_=cs[:]), reads=["cs"], dma="cst")
        S.add("sp", lambda e: e.dma_start(out=sc["sin"], in_=sn[:]), reads=["sn"], dma="snt")
        S.emit()


def stage_zg(nc, cx, h, W, sc, NTOK):
    NB = NTOK // 128
    T = 512
    NT = NTOK // T
    KC = 16
    GC = 0.044715
    GS = 1.5957691216057308
    with ExitStack() as es:
        S = Sched(nc, es)
        sb = lambda name, shape, dt: es.enter_context(nc.sbuf_tensor(f"{name}_{S.uid}", shape, dt))
        win = sb("win", [128, KC, INC], BF16)
        gB = sb("gB", [128, D], F32)
        gqa = sb("gqa", [128, 6], F32)
        gvB = sb("gvB", [128, GMW], F32)
        goB = sb("goB", [128, GMW], F32)
        wsT = sb("wsT", [128, 8, 128], BF16)
        bT = sb("bT", [128, 8], F32)
        stat = sb("stat", [128, NB, 8], F32)
        krr = sb("krr", [128, NB, 64], F32)
        xin = [sb(f"xin{i}", [128, D], F32) for i in range(2)]
        xn = sb("xn", [128, D], BF16)
        ss = sb("ss", [128, 1], F32)
        rstd = sb("rstd", [128, 1], F32)
        xnT = sb("xnT", [128, KC, T], BF16)
        cev = [sb(f"cev{i}", [128, T], BF16) for i in range(2)]
        junk = sb("junk", [128, 1024], BF16)
        tmp = [sb(f"tmp{i}", [128, 1024], F32) for i in range(2)]
        uraw = [sb(f"uraw{i}", [128, 1024], F32) for i in range(2)]
        vraw = [sb(f"vraw{i}", [128, 1024], F32) for i in range(2)]
        ug = sb("ug", [128, 1024], F32)
        vg = sb("vg", [128, 1024], F32)
        vn = sb("vn", [128, 1024], BF16)
        go = sb("go", [128, 1024], F32)
        gob = sb("gob", [128, 1024], BF16)
        goT = [sb(f"goT{i}", [128, 8, 128], BF16) for i in range(2)]
        wsf_v = tmp[0][:].rearrange("p (g s) -> p g s", g=8)
        wsb_v = gob[:].rearrange("p (g s) -> p g s", g=8)
        sst = sb("sst", [128, 4], F32)
        ps = es.enter_context(nc.psum_tensor(f"ps_{S.uid}", [128, 8, 512], F32))

        S.add("sp", lambda e: e.dma_start(out=gB[:], in_=bcast_row(W["mix_norm"], 128)),
              writes=["gB"], dma="gB")
        S.add("sp", lambda e: e.dma_start(out=gvB[:], in_=bcast_row(W["gm_v_norm"], 128)),
              writes=["gvB"], dma="gvB")
        S.add("sp", lambda e: e.dma_start(out=goB[:], in_=bcast_row(W["gm_out_norm"], 128)),
              writes=["goB"], dma="goB")
        S.add("sp", lambda e: e.dma_start(
            out=gqa[:, 0:4], in_=W["q_a_norm"].rearrange("(c p) -> p c", p=128),
            allow_slow_non_contiguous=True), writes=["gqa"], dma="gqa")
        S.add("sp", lambda e: e.dma_start(
            out=gqa[:, 4:6], in_=W["kv_a_norm"].rearrange("(c p) -> p c", p=128),
            allow_slow_non_contiguous=True), writes=["gqa"], dma="gqa")
        S.add("sp", lambda e: e.dma_start(
            out=bT[:], in_=W["gm_bs"].rearrange("g t -> t g"), allow_slow_non_contiguous=True),
            writes=["bT"], dma="bT")
        S.add("sp", lambda e: e.dma_start(out=wsf_v, in_=W["gm_ws"].rearrange("g t s -> t g s")),
              writes=["tmp0"], dma="wsf")
        S.add("pool", lambda e: e.affine_select(
            out=wsf_v, in_=wsf_v, pattern=[[0, 8], [-1, 128]], compare_op=ALU.is_ge, fill=0.0,
            base=0, channel_multiplier=1), reads=["tmp0"], writes=["tmp0"])
        S.add("dve", lambda e: e.tensor_copy(out=wsb_v, in_=wsf_v), reads=["tmp0"],
              writes=["gob"])

        def trw(e):
            pst = ps[:, 0, :].bitcast(BF16)
            ins = None
            for g in range(8):
                ins = e.transpose(out=pst[:, g * 128:(g + 1) * 128], in_=wsb_v[:, g, :],
                                  identity=cx.ident[:])
            return ins
        S.add("pe", trw, reads=["gob", "ident"], writes=["ps0"])
        S.add("dve", lambda e: e.tensor_copy(
            out=wsT[:], in_=ps[:, 0, :].bitcast(BF16).rearrange("p (g t) -> p g t", g=8)),
            reads=["ps0"], writes=["wsT"])
        wkeys = load_w_resident(S, win, W["w_in"], KC, INC, "win", chunk=576)
        wkey_of = lambda c0, c1: [f"win{c}" for c in range(c0 // 576, (c1 - 1) // 576 + 1)]

        h_t = h.rearrange("(n p) d -> n p d", p=128)
        tb = Rot([6, 7])
        xcnt = 0
        def phase0(t):
            nonlocal xcnt
            for s in range(4):
                b = xcnt % 2
                xcnt += 1
                blk = t * 4 + s
                norm_transpose(S, cx, h_t[blk], xin[b][:], xn[:], ss[:], rstd[:], gB[:],
                               xnT[:, :, s * 128:(s + 1) * 128], ps, tb, b, KC, f"xnT{s}")
        phase0(0)
        for t in range(NT):
            xkeys = [f"xnT{s}" for s in range(4)]
            for cc in range(6):
                bank = tb.next()

                def mmc(e, cc=cc, bank=bank):
                    ins = None
                    for kc in range(KC):
                        ins = e.matmul(out=ps[:, bank, :], lhsT=win[:, kc, cc * 128:(cc + 1) * 128],
                                       rhs=xnT[:, kc, :], start=(kc == 0), stop=(kc == KC - 1))
                    return ins
                S.add("pe", mmc, reads=xkeys + wkey_of(cc * 128, cc * 128 + 128),
                      writes=[f"ps{bank}"])
                cb = cc % 2
                S.add("act", lambda e, cc=cc, bank=bank, cb=cb: e.activation(
                    out=cev[cb][:], in_=ps[:, bank, :], func=AF.Copy, scale=gqa[:, cc:cc + 1]),
                    reads=[f"ps{bank}", "gqa"], writes=[f"cev{cb}"])
                dstT = sc["cqT"][cc] if cc < 4 else sc["ckvT"][cc - 4]
                S.add("sp", lambda e, dstT=dstT, cb=cb, t=t: e.dma_start(
                    out=dstT[:, t * T:(t + 1) * T], in_=cev[cb][:]),
                    reads=[f"cev{cb}"], dma=f"cevs{cb}")
            def a_mm(s, t=t):
                def mmt(e, c0, c1, banks):
                    ins = None
                    for kc in range(KC):
                        for i, bk in enumerate(banks):
                            a_ = c0 + i * 512
                            bnd = min(c1, a_ + 512)
                            ins = e.matmul(out=ps[:, bk, 0:bnd - a_],
                                           lhsT=xnT[:, kc, s * 128:(s + 1) * 128],
                                           rhs=win[:, kc, a_:bnd], start=(kc == 0),
                                           stop=(kc == KC - 1))
                    return ins
                S.add("pe", lambda e: mmt(e, 0, 832, (0, 1)),
                      reads=[f"xnT{s}"] + wkey_of(0, 832), writes=["ps0", "ps1"])
                S.add("pe", lambda e: mmt(e, 832, 1856, (2, 3)),
                      reads=[f"xnT{s}"] + wkey_of(832, 1856), writes=["ps2", "ps3"])
                S.add("pe", lambda e: mmt(e, 1856, 2880, (4, 5)),
                      reads=[f"xnT{s}"] + wkey_of(1856, 2880), writes=["ps4", "ps5"])

            def a_ev(s, t=t):
                blk = t * 4 + s
                par = blk % 2
                S.add("act", lambda e: e.copy(out=uraw[par][:].rearrange("p (a b) -> p a b", a=2),
                                              in_=ps[:, 2:4, :]),
                      reads=["ps2", "ps3"], writes=[f"uraw{par}"])
                S.add("act", lambda e: e.copy(out=vraw[par][:].rearrange("p (a b) -> p a b", a=2),
                                              in_=ps[:, 4:6, :]),
                      reads=["ps4", "ps5"], writes=[f"vraw{par}"])
                S.add("act", lambda e: e.activation(out=junk[:, 0:512], in_=ps[:, 0, :],
                                                    func=AF.Square, accum_out=sst[:, 0:1]),
                      reads=["ps0"], writes=["junk", "sst0"])
                S.add("act", lambda e: e.activation(out=junk[:, 0:256], in_=ps[:, 1, 0:256],
                                                    func=AF.Square, accum_out=sst[:, 1:2]),
                      reads=["ps1"], writes=["junk", "sst1"])
                S.add("act", lambda e: e.activation(
                    out=junk[:, 0:64], in_=ps[:, 1, 256:320], func=AF.Square,
                    accum_out=stat[:, blk, 2:3]), reads=["ps1"], writes=["junk", "stat"])
                S.add("dve", lambda e: e.tensor_copy(out=krr[:, blk, :], in_=ps[:, 1, 256:320]),
                      reads=["ps1"], writes=["krr"])
                rstd_op(S, cx, sst[:, 0:1], stat[:, blk, 0:1], QL, ["sst0"], ["stat"])
                rstd_op(S, cx, sst[:, 1:2], stat[:, blk, 1:2], KVL, ["sst1"], ["stat"])

            def b_chain(s, t=t):
                blk = t * 4 + s
                par = blk % 2
                chains = ((uraw[par][:], f"uraw{par}", ug[:], "ug", tmp[0][:], ["tmp0"]),
                          (vraw[par][:], f"vraw{par}", vg[:], "vg", tmp[1][:], ["tmp1"]))
                for (xs, rk, dv, name, tm, tk) in chains:
                    S.add("act", lambda e, xs=xs, tm=tm: e.activation(out=tm, in_=xs, func=AF.Square),
                          reads=[rk], writes=tk)
                for (xs, rk, dv, name, tm, tk) in chains:
                    S.add("dve", lambda e, tm=tm: e.tensor_scalar(
                        out=tm, in0=tm, scalar1=GC, scalar2=1.0, op0=ALU.mult, op1=ALU.add),
                        reads=tk, writes=tk)
                    S.add("dve", lambda e, tm=tm, xs=xs: e.tensor_tensor(
                        out=tm, in0=tm, in1=xs, op=ALU.mult), reads=tk + [rk], writes=tk)
                for (xs, rk, dv, name, tm, tk) in chains:
                    S.add("act", lambda e, tm=tm: e.activation(out=tm, in_=tm, func=AF.Sigmoid,
                                                               scale=GS), reads=tk, writes=tk)
                for (xs, rk, dv, name, tm, tk) in reversed(chains):
                    S.add("dve", lambda e, tm=tm, xs=xs, dv=dv: e.tensor_tensor(
                        out=dv, in0=tm, in1=xs, op=ALU.mult), reads=tk + [rk], writes=[name])
                S.add("act", lambda e: e.activation(out=junk[:], in_=vg[:], func=AF.Square,
                                                    accum_out=sst[:, 2:3]),
                      reads=["vg"], writes=["junk", "sst2"])
                rstd_op(S, cx, sst[:, 2:3], sst[:, 2:3], GMW, ["sst2"], ["sst2"])
                S.add("dve", lambda e: e.scalar_tensor_tensor(
                    out=vn[:], in0=vg[:], scalar=sst[:, 2:3], in1=gvB[:], op0=ALU.mult,
                    op1=ALU.mult), reads=["vg", "sst2", "gvB"], writes=["vn"])

                def mmg(e):
                    ins = None
                    for g in range(8):
                        ins = e.matmul(out=ps[:, 6 + g // 4, (g % 4) * 128:(g % 4 + 1) * 128],
                                       lhsT=wsT[:, g, :], rhs=vn[:, g * 128:(g + 1) * 128],
                                       start=True, stop=True)
                    return ins
                S.add("pe", mmg, reads=["vn", "wsT"], writes=["ps6", "ps7"])
                S.add("dve", lambda e: e.tensor_tensor(
                    out=go[:].rearrange("p (g d) -> p g d", g=8),
                    in0=ps[:, 6:8, :].rearrange("p a (g d) -> p (a g) d", g=4),
                    in1=bT[:].unsqueeze(2).to_broadcast([128, 8, 128]), op=ALU.add),
                    reads=["ps6", "ps7", "bT"], writes=["go"])
                S.add("dve", lambda e: e.tensor_tensor(out=go[:], in0=go[:], in1=ug[:],
                                                       op=ALU.mult),
                      reads=["go", "ug"], writes=["go"])
                S.add("act", lambda e: e.activation(out=junk[:], in_=go[:], func=AF.Square,
                                                    accum_out=sst[:, 3:4]),
                      reads=["go"], writes=["junk", "sst3"])
                rstd_op(S, cx, sst[:, 3:4], stat[:, blk, 3:4], GMW, ["sst3"], ["stat"])
                S.add("dve", lambda e: e.tensor_tensor(out=gob[:], in0=go[:], in1=goB[:],
                                                       op=ALU.mult),
                      reads=["go", "goB"], writes=["gob"])
                bank = tb.next()
                gb = blk % 2

                def trg(e, bank=bank):
                    pst = ps[:, bank, :].bitcast(BF16)
                    ins = None
                    for c in range(8):
                        ins = e.transpose(out=pst[:, c * 128:(c + 1) * 128],
                                          in_=gob[:, c * 128:(c + 1) * 128], identity=cx.ident[:])
                    return ins
                S.add("pe", trg, reads=["gob", "ident"], writes=[f"ps{bank}"])
                S.add("act", lambda e, bank=bank, gb=gb: e.copy(
                    out=goT[gb][:], in_=ps[:, bank, :].bitcast(BF16).rearrange(
                        "p (c t) -> p c t", c=8)), reads=[f"ps{bank}"], writes=[f"goT{gb}"])
                S.add("sp", lambda e, gb=gb, blk=blk: e.dma_start(
                    out=sc["mixT"][8:16].rearrange("c p n -> p c n")[:, :, blk * 128:(blk + 1) * 128],
                    in_=goT[gb][:]), reads=[f"goT{gb}"], dma=f"goTs{gb}")

            a_mm(0)
            a_ev(0)
            for s in range(4):
                if s + 1 < 4:
                    a_mm(s + 1)
                if s == 2 and t + 1 < NT:
                    phase0(t + 1)
                b_chain(s)
                if s + 1 < 4:
                    a_ev(s + 1)
        S.add("sp", lambda e: e.dma_start(out=sc["stat"], in_=stat[:]), reads=["stat"], dma="stats")
        S.add("sp", lambda e: e.dma_start(out=sc["krr"], in_=krr[:]), reads=["krr"], dma="krrs")
        S.emit()


def stage_qk(nc, cx, W, sc, NTOK):
    NB = NTOK // 128
    SCALE = 192.0 ** -0.5
    with ExitStack() as es:
        S = Sched(nc, es)
        sb = lambda name, shape, dt: es.enter_context(nc.sbuf_tensor(f"{name}_{S.uid}", shape, dt))
        wuq = sb("wuq", [128, 4, 1536], BF16)
        wk = sb("wk", [128, 2, 8, 128], BF16)
        gq = sb("gq", [128, 192], F32)
        gk = sb("gk", [128, 192], F32)
        stat = sb("stat", [128, NB, 8], F32)
        krr = sb("krr", [128, NB, 64], F32)
        cs = sb("cs", [128, NB, 32], F32)
        sn = sb("sn", [128, NB, 32], F32)
        ksc = sb("ksc", [128, NB, 8], F32)
        cq = [sb(f"cq{i}", [128, 4, 512], BF16) for i in range(2)]
        ckv = [sb(f"ckv{i}", [128, 2, 512], BF16) for i in range(2)]
        sq = sb("sq", [128, 1536], F32)
        ssq = sb("ssq", [128, 8], F32)
        ssk = sb("ssk", [128, 8], F32)
        rt = sb("rt", [128, 8], F32)
        r2 = sb("r2", [128, 1], F32)
        qn = sb("qn", [128, 8, 192], F32)
        qb = sb("qb", [128, 8, 192], BF16)
        kb = sb("kb", [128, 8, 128], BF16)
        krn = sb("krn", [128, 64], F32)
        krb = sb("krb", [128, 64], BF16)
        ra = sb("ra", [128, 8, 32], F32)
        rb = sb("rb", [128, 8, 32], F32)
        qTn = [sb(f"qTn{i}", [128, 8, 128], BF16) for i in range(2)]
        qTr = [sb(f"qTr{i}", [64, 8, 128], BF16) for i in range(2)]
        kTn = [sb(f"kTn{i}", [128, 8, 128], BF16) for i in range(2)]
        kTr = [sb(f"kTr{i}", [64, 128], BF16) for i in range(2)]
        ps = es.enter_context(nc.psum_tensor(f"ps_{S.uid}", [128, 8, 512], F32))

        for (tl, src, nm) in ((stat, sc["stat"], "stat"), (krr, sc["krr"], "krr"),
                              (cs, sc["cos"], "cs"), (sn, sc["sin"], "sn")):
            S.add("sp", lambda e, tl=tl, src=src: e.dma_start(out=tl[:], in_=src),
                  writes=[nm], dma=nm)
        S.add("sp", lambda e: e.dma_start(out=gq[:], in_=bcast_row(W["q_norm"], 128)),
              writes=["gq"], dma="gq")
        S.add("sp", lambda e: e.dma_start(out=gk[:], in_=bcast_row(W["k_norm"], 128)),
              writes=["gk"], dma="gk")
        load_w_resident(S, wuq, W["w_uq"], 4, 1536, "wuq", chunk=1536)
        wkv = W["w_ukv"].rearrange("(kc p) (h j) -> p kc h j", p=128, j=256)
        for kc in range(2):
            S.add("pool", lambda e, kc=kc: e.dma_start(out=wk[:, kc], in_=wkv[:, kc, :, 0:128]),
                  writes=["wk"], dma=f"wk{kc}")

        cqv = sc["cqT"].rearrange("c p n -> p c n")
        ckvv = sc["ckvT"].rearrange("c p n -> p c n")
        for blk in range(NB):
            t, s = divmod(blk, 4)
            cbuf = t % 2
            if s == 0:
                S.add("sp", lambda e, t=t, cbuf=cbuf: e.dma_start(
                    out=cq[cbuf][:], in_=cqv[:, :, t * 512:(t + 1) * 512]),
                    writes=[f"cq{cbuf}"], dma=f"cq{cbuf}")
                S.add("sp", lambda e, t=t, cbuf=cbuf: e.dma_start(
                    out=ckv[cbuf][:], in_=ckvv[:, :, t * 512:(t + 1) * 512]),
                    writes=[f"ckv{cbuf}"], dma=f"ckv{cbuf}")
            tok = slice(s * 128, (s + 1) * 128)

            def mq(e, cbuf=cbuf, tok=tok):
                ins = None
                for g in range(3):
                    for kc in range(4):
                        ins = e.matmul(out=ps[:, g, :], lhsT=cq[cbuf][:, kc, tok],
                                       rhs=wuq[:, kc, g * 512:(g + 1) * 512],
                                       start=(kc == 0), stop=(kc == 3))
                return ins
            S.add("pe", mq, reads=[f"cq{cbuf}", "wuq0"], writes=["ps0", "ps1", "ps2"])

            def mk(e, cbuf=cbuf, tok=tok):
                ins = None
                for g in range(2):
                    for kc in range(2):
                        ins = e.matmul(out=ps[:, 3 + g, :], lhsT=ckv[cbuf][:, kc, tok],
                                       rhs=wk[:, kc, g * 4:(g + 1) * 4, :].rearrange("p h j -> p (h j)"),
                                       start=(kc == 0), stop=(kc == 1))
                return ins
            S.add("pe", mk, reads=[f"ckv{cbuf}", "wk"], writes=["ps3", "ps4"])
            qps = ps[:, 0:3, :]
            qv = ps[:, 0:3, :].rearrange("p a b -> p (a b)").rearrange("p (h d) -> p h d", h=8)
            kps = ps[:, 3:5, :]
            kv_ = ps[:, 3:5, :].rearrange("p a b -> p (a b)").rearrange("p (h d) -> p h d", h=8)
            S.add("act", lambda e, qps=qps: e.activation(
                out=sq[:].rearrange("p (a b) -> p a b", a=3), in_=qps, func=AF.Square),
                reads=["ps0", "ps1", "ps2"], writes=["sq"])
            S.add("dve", lambda e: e.reduce_sum(
                out=ssq[:], in_=sq[:].rearrange("p (h d) -> p h d", h=8), axis=AX.X),
                reads=["sq"], writes=["ssq"])
            S.add("dve", lambda e, blk=blk: e.tensor_tensor(
                out=r2[:], in0=stat[:, blk, 0:1], in1=stat[:, blk, 0:1], op=ALU.mult),
                reads=["stat"], writes=["r2"])
            S.add("dve", lambda e: e.tensor_scalar(
                out=ssq[:], in0=ssq[:], scalar1=r2[:, 0:1], scalar2=None, op0=ALU.mult),
                reads=["ssq", "r2"], writes=["ssq"])
            rstd_op(S, cx, ssq[:], rt[:], 192.0, ["ssq"], ["rt"])
            S.add("dve", lambda e, blk=blk: e.tensor_scalar(
                out=rt[:], in0=rt[:], scalar1=stat[:, blk, 0:1], scalar2=None, op0=ALU.mult),
                reads=["rt", "stat"], writes=["rt"])
            S.add("dve", lambda e, qv=qv: e.tensor_tensor(
                out=qn[:], in0=qv, in1=rt[:].unsqueeze(2).to_broadcast([128, 8, 192]),
                op=ALU.mult), reads=["ps0", "ps1", "ps2", "rt"], writes=["qn"])
            S.add("dve", lambda e: e.tensor_tensor(
                out=qn[:], in0=qn[:], in1=gq[:].unsqueeze(1).to_broadcast([128, 8, 192]),
                op=ALU.mult), reads=["qn", "gq"], writes=["qn"])
            S.add("act", lambda e: e.copy(out=qb[:, :, 0:128], in_=qn[:, :, 0:128]),
                  reads=["qn"], writes=["qbn"])
            csb = cs[:, blk, :].unsqueeze(1).to_broadcast([128, 8, 32])
            snb = sn[:, blk, :].unsqueeze(1).to_broadcast([128, 8, 32])
            x1 = qn[:, :, 128:160]
            x2 = qn[:, :, 160:192]
            S.add("dve", lambda e, x1=x1, csb=csb: e.tensor_tensor(out=ra[:], in0=x1, in1=csb,
                                                                   op=ALU.mult),
                  reads=["qn", "cs"], writes=["ra"])
            S.add("dve", lambda e, x2=x2, snb=snb: e.tensor_tensor(out=rb[:], in0=x2, in1=snb,
                                                                   op=ALU.mult),
                  reads=["qn", "sn"], writes=["rb"])
            S.add("dve", lambda e: e.tensor_tensor(out=qb[:, :, 128:160], in0=ra[:], in1=rb[:],
                                                   op=ALU.subtract),
                  reads=["ra", "rb"], writes=["qbr1"])
            S.add("dve", lambda e, x2=x2, csb=csb: e.tensor_tensor(out=ra[:], in0=x2, in1=csb,
                                                                   op=ALU.mult),
                  reads=["qn", "cs"], writes=["ra"])
            S.add("dve", lambda e, x1=x1, snb=snb: e.tensor_tensor(out=rb[:], in0=x1, in1=snb,
                                                                   op=ALU.mult),
                  reads=["qn", "sn"], writes=["rb"])
            S.add("dve", lambda e: e.tensor_tensor(out=qb[:, :, 160:192], in0=ra[:], in1=rb[:],
                                                   op=ALU.add),
                  reads=["ra", "rb"], writes=["qbr2"])
            S.add("act", lambda e, kps=kps: e.activation(
                out=sq[:, 0:1024].rearrange("p (a b) -> p a b", a=2), in_=kps, func=AF.Square),
                reads=["ps3", "ps4"], writes=["sq"])
            S.add("dve", lambda e: e.reduce_sum(
                out=ssk[:], in_=sq[:, 0:1024].rearrange("p (h d) -> p h d", h=8), axis=AX.X),
                reads=["sq"], writes=["ssk"])
            S.add("dve", lambda e, blk=blk: e.tensor_tensor(
                out=r2[:], in0=stat[:, blk, 1:2], in1=stat[:, blk, 1:2], op=ALU.mult),
                reads=["stat"], writes=["r2"])
            S.add("dve", lambda e, blk=blk: e.tensor_scalar(
                out=ssk[:], in0=ssk[:], scalar1=r2[:, 0:1], scalar2=stat[:, blk, 2:3],
                op0=ALU.mult, op1=ALU.add), reads=["ssk", "r2", "stat"], writes=["ssk"])
            rstd_op(S, cx, ssk[:], ssk[:], 192.0, ["ssk"], ["ssk"])
            S.add("dve", lambda e, blk=blk: e.tensor_scalar(
                out=ksc[:, blk, :], in0=ssk[:], scalar1=stat[:, blk, 1:2], scalar2=SCALE,
                op0=ALU.mult, op1=ALU.mult), reads=["ssk", "stat"], writes=["ksc"])
            S.add("dve", lambda e, kv_=kv_: e.tensor_tensor(
                out=kb[:], in0=kv_, in1=gk[:, 0:128].unsqueeze(1).to_broadcast([128, 8, 128]),
                op=ALU.mult), reads=["ps3", "ps4", "gk"], writes=["kb"])
            S.add("dve", lambda e, blk=blk: e.tensor_tensor(
                out=krn[:], in0=krr[:, blk, :], in1=gk[:, 128:192], op=ALU.mult),
                reads=["krr", "gk"], writes=["krn"])
            S.add("dve", lambda e, blk=blk: e.reciprocal(out=r2[:], in_=stat[:, blk, 1:2]),
                  reads=["stat"], writes=["r2"])
            S.add("dve", lambda e, blk=blk: e.tensor_scalar(
                out=krn[:], in0=krn[:], scalar1=r2[:, 0:1], scalar2=None, op0=ALU.mult),
                reads=["krn", "r2"], writes=["krn"])
            c1 = cs[:, blk, :]
            s1 = sn[:, blk, :]
            S.add("dve", lambda e, c1=c1: e.tensor_tensor(out=ra[:, 0, :], in0=krn[:, 0:32], in1=c1,
                                                          op=ALU.mult),
                  reads=["krn", "cs"], writes=["ra"])
            S.add("dve", lambda e, s1=s1: e.tensor_tensor(out=rb[:, 0, :], in0=krn[:, 32:64], in1=s1,
                                                          op=ALU.mult),
                  reads=["krn", "sn"], writes=["rb"])
            S.add("dve", lambda e: e.tensor_tensor(out=krb[:, 0:32], in0=ra[:, 0, :], in1=rb[:, 0, :],
                                                   op=ALU.subtract),
                  reads=["ra", "rb"], writes=["krb1"])
            S.add("dve", lambda e, c1=c1: e.tensor_tensor(out=ra[:, 0, :], in0=krn[:, 32:64], in1=c1,
                                                          op=ALU.mult),
                  reads=["krn", "cs"], writes=["ra"])
            S.add("dve", lambda e, s1=s1: e.tensor_tensor(out=rb[:, 0, :], in0=krn[:, 0:32], in1=s1,
                                                          op=ALU.mult),
                  reads=["krn", "sn"], writes=["rb"])
            S.add("dve", lambda e: e.tensor_tensor(out=krb[:, 32:64], in0=ra[:, 0, :], in1=rb[:, 0, :],
                                                   op=ALU.add),
                  reads=["ra", "rb"], writes=["krb2"])
            ob = blk % 2

            def trq(e):
                p5 = ps[:, 5, :].bitcast(BF16)
                p6 = ps[:, 6, :].bitcast(BF16)
                p7 = ps[:, 7, :].bitcast(BF16)
                ins = None
                for hh in range(8):
                    ins = e.transpose(out=p5[:, hh * 128:(hh + 1) * 128], in_=qb[:, hh, 0:128],
                                      identity=cx.ident[:])
                for hh in range(8):
                    ins = e.transpose(out=p6[0:64, hh * 128:(hh + 1) * 128], in_=qb[:, hh, 128:192],
                                      identity=cx.ident[:])
                for hh in range(8):
                    ins = e.transpose(out=p7[:, hh * 128:(hh + 1) * 128], in_=kb[:, hh, :],
                                      identity=cx.ident[:])
                return ins
            S.add("pe", trq, reads=["qbn", "qbr1", "qbr2", "kb", "ident"],
                  writes=["ps5", "ps6", "ps7"])
            S.add("act", lambda e, ob=ob: e.copy(
                out=qTn[ob][:], in_=ps[:, 5, :].bitcast(BF16).rearrange("p (h t) -> p h t", h=8)),
                reads=["ps5"], writes=[f"qTn{ob}"])
            S.add("dve", lambda e, ob=ob: e.tensor_copy(
                out=qTr[ob][:], in_=ps[0:64, 6, :].bitcast(BF16).rearrange("p (h t) -> p h t", h=8)),
                reads=["ps6"], writes=[f"qTr{ob}"])
            S.add("act", lambda e, ob=ob: e.copy(
                out=kTn[ob][:], in_=ps[:, 7, :].bitcast(BF16).rearrange("p (h t) -> p h t", h=8)),
                reads=["ps7"], writes=[f"kTn{ob}"])

            def trk(e):
                p6 = ps[:, 6, :].bitcast(BF16)
                return e.transpose(out=p6[0:64, 0:128], in_=krb[:, :], identity=cx.ident[:])
            S.add("pe", trk, reads=["krb1", "krb2", "ident"], writes=["ps6"])
            S.add("dve", lambda e, ob=ob: e.tensor_copy(
                out=kTr[ob][:], in_=ps[0:64, 6, :].bitcast(BF16)[:, 0:128]),
                reads=["ps6"], writes=[f"kTr{ob}"])
            tsl = slice(blk * 128, (blk + 1) * 128)
            S.add("sp", lambda e, ob=ob, tsl=tsl: e.dma_start(
                out=sc["qTn"].rearrange("h p n -> p h n")[:, :, tsl], in_=qTn[ob][:]),
                reads=[f"qTn{ob}"], dma=f"qTns{ob}")
            S.add("sp", lambda e, ob=ob, tsl=tsl: e.dma_start(
                out=sc["qTr"].rearrange("h p n -> p h n")[:, :, tsl], in_=qTr[ob][:]),
                reads=[f"qTr{ob}"], dma=f"qTrs{ob}")
            S.add("sp", lambda e, ob=ob, tsl=tsl: e.dma_start(
                out=sc["kTn"].rearrange("h p n -> p h n")[:, :, tsl], in_=kTn[ob][:]),
                reads=[f"kTn{ob}"], dma=f"kTns{ob}")
            S.add("sp", lambda e, ob=ob, tsl=tsl: e.dma_start(
                out=sc["kTr"][:, tsl], in_=kTr[ob][:]),
                reads=[f"kTr{ob}"], dma=f"kTrs{ob}")
        S.add("sp", lambda e: e.dma_start(out=sc["ksc"], in_=ksc[:]), reads=["ksc"], dma="kscs")
        S.emit()


def stage_attn(nc, cx, W, sc, NTOK):
    NB = NTOK // 128
    NQT = NTOK // 512
    with ExitStack() as es:
        S = Sched(nc, es)
        sb = lambda name, shape, dt: es.enter_context(nc.sbuf_tensor(f"{name}_{S.uid}", shape, dt))
        ckv = sb("ckv", [128, 2, NTOK], BF16)
        wv = sb("wv", [128, 2, 8, 128], BF16)
        stat = sb("stat", [128, NB, 8], F32)
        ksc = sb("ksc", [128, NB, 8], F32)
        gaB = sb("gaB", [128, 1024], F32)
        kTr = sb("kTr", [64, NTOK], BF16)
        qTn = [sb(f"qTn{i}", [128, NTOK], BF16) for i in range(2)]
        qTr = [sb(f"qTr{i}", [64, NTOK], BF16) for i in range(2)]
        kTn = [sb(f"kTn{i}", [128, NTOK], BF16) for i in range(2)]
        vaug = [sb(f"vaug{i}", [128, NB, 132], BF16) for i in range(2)]
        aT = [sb(f"aT{i}", [128, NTOK], BF16) for i in range(2)]
        pT = [sb(f"pT{i}", [128, 512], BF16) for i in range(4)]
        tri = sb("tri", [128, 128], BF16)
        trif = sb("trif", [128, 128], F32)
        ssa = sb("ssa", [128, NB, 8], F32)
        rden = [sb(f"rden{i}", [128, 1], F32) for i in range(2)]
        af = [sb(f"af{i}", [128, 128], F32) for i in range(2)]
        ab = [sb(f"ab{i}", [128, 128], BF16) for i in range(2)]
        junk = sb("junk", [128, 128], BF16)
        ps = es.enter_context(nc.psum_tensor(f"ps_{S.uid}", [128, 8, 512], F32))

        S.add("sp", lambda e: e.dma_start(out=stat[:], in_=sc["stat"]), writes=["stat"], dma="stat")
        S.add("sp", lambda e: e.dma_start(out=ksc[:], in_=sc["ksc"]), writes=["ksc"], dma="ksc")
        S.add("sp", lambda e: e.dma_start(out=gaB[:], in_=bcast_row(W["attn_out_norm"], 128)),
              writes=["gaB"], dma="gaB")
        S.add("sp", lambda e: e.dma_start(out=kTr[:], in_=sc["kTr"]), writes=["kTr"], dma="kTr")
        S.add("sp", lambda e: e.dma_start(out=ckv[:], in_=sc["ckvT"].rearrange("c p n -> p c n")),
              writes=["ckv"], dma="ckv")
        wkv = W["w_ukv"].rearrange("(kc p) (h j) -> p kc h j", p=128, j=256)
        for kc in range(2):
            S.add("pool", lambda e, kc=kc: e.dma_start(out=wv[:, kc], in_=wkv[:, kc, :, 128:256]),
                  writes=["wv"], dma=f"wv{kc}")
        def mktri(e):
            e.memset(trif[:], 1.0)
            return e.affine_select(out=trif[:], in_=trif[:], pattern=[[1, 128]],
                                   compare_op=ALU.is_ge, fill=0.0, base=0, channel_multiplier=-1)
        S.add("pool", mktri, writes=["trif"])
        S.add("dve", lambda e: e.tensor_copy(out=tri[:], in_=trif[:]), reads=["trif"],
              writes=["tri"])
        for i in range(2):
            S.add("dve", lambda e, i=i: e.memset(vaug[i][:, :, 128:132], 1.0),
                  writes=[f"vaug{i}"])
        sbank = Rot([0, 1, 2, 3])
        pcnt = 0
        ecnt = 0
        for h in range(NH):
            hb_ = h % 2
            S.add("sp", lambda e, h=h, hb_=hb_: e.dma_start(out=qTn[hb_][:], in_=sc["qTn"][h]),
                  writes=[f"qTn{hb_}"], dma=f"qTn{hb_}")
            S.add("sp", lambda e, h=h, hb_=hb_: e.dma_start(out=qTr[hb_][:], in_=sc["qTr"][h]),
                  writes=[f"qTr{hb_}"], dma=f"qTr{hb_}")
            S.add("sp", lambda e, h=h, hb_=hb_: e.dma_start(out=kTn[hb_][:], in_=sc["kTn"][h]),
                  writes=[f"kTn{hb_}"], dma=f"kTn{hb_}")
            for blk in range(NB):
                bank = sbank.next()

                def mv(e, blk=blk, bank=bank, h=h):
                    ins = None
                    for kc in range(2):
                        ins = e.matmul(out=ps[:, bank, 0:128],
                                       lhsT=ckv[:, kc, blk * 128:(blk + 1) * 128],
                                       rhs=wv[:, kc, h, :], start=(kc == 0), stop=(kc == 1))
                    return ins
                S.add("pe", mv, reads=["ckv", "wv"], writes=[f"ps{bank}"])
                S.add("act", lambda e, blk=blk, bank=bank, hb_=hb_: e.activation(
                    out=vaug[hb_][:, blk, 0:128], in_=ps[:, bank, 0:128], func=AF.Copy,
                    scale=stat[:, blk, 1:2]), reads=[f"ps{bank}", "stat"], writes=[f"vaug{hb_}"])
            items = [(qt, kbi) for qt in range(NQT) for kbi in range(4 * qt + 4)]
            info = {}
            LOOK = 3

            def emit_s(it, h=h, hb_=hb_):
                nonlocal pcnt
                qt, kbi = it
                q0 = qt * 512
                j = kbi - 4 * qt
                off = 128 * j if j > 0 else 0
                n = 512 - off
                bank = sbank.next()
                ksl = slice(kbi * 128, (kbi + 1) * 128)

                def ms(e, bank=bank, ksl=ksl, off=off, n=n, q0=q0):
                    e.matmul(out=ps[:, bank, 0:n], lhsT=kTn[hb_][:, ksl],
                             rhs=qTn[hb_][:, q0 + off:q0 + 512], start=True, stop=False)
                    return e.matmul(out=ps[:, bank, 0:n], lhsT=kTr[:, ksl],
                                    rhs=qTr[hb_][:, q0 + off:q0 + 512], start=False, stop=True)
                S.add("pe", ms, reads=[f"kTn{hb_}", f"qTn{hb_}", f"qTr{hb_}", "kTr"],
                      writes=[f"ps{bank}"])
                pb = pcnt % 4
                pcnt += 1
                S.add("act", lambda e, bank=bank, n=n, pb=pb, kbi=kbi: e.activation(
                    out=pT[pb][:, 0:n], in_=ps[:, bank, 0:n], func=AF.Exp,
                    scale=ksc[:, kbi, h:h + 1]), reads=[f"ps{bank}", "ksc"], writes=[f"pT{pb}"])
                if j >= 0:
                    S.add("dve", lambda e, pb=pb: e.tensor_tensor(
                        out=pT[pb][:, 0:128], in0=pT[pb][:, 0:128], in1=tri[:], op=ALU.mult),
                        reads=[f"pT{pb}", "tri"], writes=[f"pT{pb}"])
                info[it] = (pb, off, j)

            def emit_pv(it, h=h, hb_=hb_):
                nonlocal ecnt
                qt, kbi = it
                pb, off, j = info.pop(it)
                qs0 = j if j > 0 else 0

                def mpv(e, pb=pb, kbi=kbi, qs0=qs0, off=off, qt=qt):
                    ins = None
                    for qs in range(qs0, 4):
                        c0 = qs * 128 - off
                        ins = e.matmul(out=ps[:, 4 + qs, 0:129], lhsT=pT[pb][:, c0:c0 + 128],
                                       rhs=vaug[hb_][:, kbi, 0:129], start=(kbi == 0),
                                       stop=(kbi == 4 * qt + qs))
                    return ins
                S.add("pe", mpv, reads=[f"pT{pb}", f"vaug{hb_}"],
                      writes=[f"ps{4 + qs}" for qs in range(qs0, 4)])
                if kbi != 4 * qt + 3:
                    return
                for qs in range(4):
                    blk = qt * 4 + qs
                    eb = ecnt % 2
                    ecnt += 1
                    acc = ps[:, 4 + qs, :]
                    S.add("dve", lambda e, acc=acc, eb=eb: e.reciprocal(out=rden[eb][:],
                                                                        in_=acc[:, 128:129]),
                          reads=[f"ps{4 + qs}"], writes=[f"rden{eb}"])
                    S.add("act", lambda e, acc=acc, eb=eb: e.activation(
                        out=af[eb][:], in_=acc[:, 0:128], func=AF.Copy, scale=rden[eb][:, 0:1]),
                        reads=[f"ps{4 + qs}", f"rden{eb}"], writes=[f"af{eb}"])
                    S.add("act", lambda e, eb=eb, blk=blk: e.activation(
                        out=junk[:], in_=af[eb][:], func=AF.Square, accum_out=ssa[:, blk, h:h + 1]),
                        reads=[f"af{eb}"], writes=["junk", "ssa"])
                    S.add("dve", lambda e, eb=eb: e.tensor_tensor(
                        out=ab[eb][:], in0=af[eb][:], in1=gaB[:, h * 128:(h + 1) * 128],
                        op=ALU.mult), reads=[f"af{eb}", "gaB"], writes=[f"ab{eb}"])
                    bank = sbank.next()
                    S.add("pe", lambda e, eb=eb, bank=bank: e.transpose(
                        out=ps[:, bank, :].bitcast(BF16)[:, 0:128], in_=ab[eb][:],
                        identity=cx.ident[:]), reads=[f"ab{eb}", "ident"], writes=[f"ps{bank}"])
                    S.add("dve", lambda e, bank=bank, blk=blk: e.tensor_copy(
                        out=aT[hb_][:, blk * 128:(blk + 1) * 128],
                        in_=ps[:, bank, :].bitcast(BF16)[:, 0:128]),
                        reads=[f"ps{bank}"], writes=[f"aT{hb_}"])

            for i in range(len(items) + LOOK):
                if i < len(items):
                    emit_s(items[i])
                if i >= LOOK:
                    emit_pv(items[i - LOOK])
            S.add("sp", lambda e, h=h, hb_=hb_: e.dma_start(out=sc["mixT"][h], in_=aT[hb_][:]),
                  reads=[f"aT{hb_}"], dma=f"aTs{hb_}")
        S.add("dve", lambda e: e.reduce_sum(out=stat[:, :, 4:5], in_=ssa[:], axis=AX.X),
              reads=["ssa", "stat"], writes=["stat"])
        rstd_op(S, cx, stat[:, :, 4:5], stat[:, :, 4:5], 1024.0, ["stat"], ["stat"])
        S.add("sp", lambda e: e.dma_start(out=sc["stat"], in_=stat[:]), reads=["stat"], dma="stats")
        S.emit()


def stage_wout(nc, cx, h, W, sc, NTOK):
    NB = NTOK // 128
    T = 512
    NT = NTOK // T
    with ExitStack() as es:
        S = Sched(nc, es)
        sb = lambda name, shape, dt: es.enter_context(nc.sbuf_tensor(f"{name}_{S.uid}", shape, dt))
        wo = sb("wo", [128, 16, D], BF16)
        stat = sb("stat", [128, NB, 8], F32)
        mx = [sb(f"mx{i}", [128, 16, T], BF16) for i in range(2)]
        hb = [sb(f"hb{i}", [128, 512], F32) for i in range(4)]
        ps = es.enter_context(nc.psum_tensor(f"ps_{S.uid}", [128, 8, 512], F32))
        S.add("sp", lambda e: e.dma_start(out=stat[:], in_=sc["stat"]), writes=["stat"], dma="stat")
        load_w_resident(S, wo, W["w_out"], 16, D, "wo", chunk=512)
        h_t = h.rearrange("(n p) d -> n p d", p=128)
        mv = sc["mixT"].rearrange("c p n -> p c n")
        pb = Rot([0, 1, 2, 3])
        hcnt = 0
        for t in range(NT):
            mb = t % 2
            for half in range(2):
                S.add("sp", lambda e, t=t, mb=mb, half=half: e.dma_start(
                    out=mx[mb][:, half * 8:(half + 1) * 8, :],
                    in_=mv[:, half * 8:(half + 1) * 8, t * T:(t + 1) * T]),
                    writes=[f"mx{mb}"], dma=f"mx{mb}")
            its = [(s, g) for s in range(4) for g in range(4)]

            def ld(k, t=t):
                s_, g_ = its[k]
                blk_ = t * 4 + s_
                hbi_ = k % 4
                S.add("sp", lambda e: e.dma_start(
                    out=hb[hbi_][:], in_=h_t[blk_][:, g_ * 512:(g_ + 1) * 512]),
                    writes=[f"hb{hbi_}"], dma=f"hbl{hbi_}")
            ld(0)
            ld(1)
            for k, (s, g) in enumerate(its):
                blk = t * 4 + s
                pa = pb.next()
                bank_a = pa
                bank_g = pa + 4

                def mm(e, mb=mb, s=s, g=g, bank_a=bank_a, bank_g=bank_g):
                    ins = None
                    for c in range(16):
                        bk = bank_a if c < 8 else bank_g
                        ins = e.matmul(out=ps[:, bk, :], lhsT=mx[mb][:, c, s * 128:(s + 1) * 128],
                                       rhs=wo[:, c, g * 512:(g + 1) * 512],
                                       start=(c % 8 == 0), stop=(c % 8 == 7))
                    return ins
                S.add("pe", mm, reads=[f"mx{mb}", f"wo{g}"],
                      writes=[f"ps{bank_a}", f"ps{bank_g}"])
                hbi = k % 4
                S.add("dve", lambda e, hbi=hbi, bank_a=bank_a, blk=blk: e.scalar_tensor_tensor(
                    out=hb[hbi][:], in0=ps[:, bank_a, :], scalar=stat[:, blk, 4:5],
                    in1=hb[hbi][:], op0=ALU.mult, op1=ALU.add),
                    reads=[f"ps{bank_a}", f"hb{hbi}", "stat"], writes=[f"hb{hbi}"])
                S.add("dve", lambda e, hbi=hbi, bank_g=bank_g, blk=blk: e.scalar_tensor_tensor(
                    out=hb[hbi][:], in0=ps[:, bank_g, :], scalar=stat[:, blk, 3:4],
                    in1=hb[hbi][:], op0=ALU.mult, op1=ALU.add),
                    reads=[f"ps{bank_g}", f"hb{hbi}", "stat"], writes=[f"hb{hbi}"])
                if k + 2 < len(its):
                    ld(k + 2)
                S.add("sp", lambda e, hbi=hbi, blk=blk, g=g: e.dma_start(
                    out=h_t[blk][:, g * 512:(g + 1) * 512], in_=hb[hbi][:]),
                    reads=[f"hb{hbi}"], dma=f"hbs{hbi}")
        S.emit()


def stage_ple(nc, cx, h, p_in, W, NTOK):
    NB = NTOK // 128
    with ExitStack() as es:
        S = Sched(nc, es)
        sb = lambda name, shape, dt: es.enter_context(nc.sbuf_tensor(f"{name}_{S.uid}", shape, dt))
        wg = sb("wg", [128, 16, D], BF16)
        wp = sb("wp", [128, 2, D], BF16)
        gB = sb("gB", [128, D], F32)
        peB = sb("peB", [128, D], F32)
        xin = [sb(f"xin{i}", [128, D], F32) for i in range(2)]
        xn = sb("xn", [128, D], BF16)
        ss = sb("ss", [128, 1], F32)
        rstd = sb("rstd", [128, 1], F32)
        xnT = [sb(f"xnT{i}", [128, 16, 128], BF16) for i in range(2)]
        pin = [sb(f"pin{i}", [128, PLE], F32) for i in range(2)]
        pbf = sb("pbf", [128, PLE], BF16)
        pT = [sb(f"pT{i}", [128, 2, 128], BF16) for i in range(2)]
        sse = sb("sse", [128, 4], F32)
        re_ = sb("re", [128, 1], F32)
        junk = sb("junk", [128, 512], BF16)
        sg = [sb(f"sg{i}", [128, 512], F32) for i in range(2)]
        t1 = [sb(f"t1{i}", [128, 512], F32) for i in range(2)]
        ps = es.enter_context(nc.psum_tensor(f"ps_{S.uid}", [128, 8, 512], F32))
        S.add("sp", lambda e: e.dma_start(out=gB[:], in_=bcast_row(W["ple_gate_norm"], 128)),
              writes=["gB"], dma="gB")
        S.add("sp", lambda e: e.dma_start(out=peB[:], in_=bcast_row(W["ple_norm"], 128)),
              writes=["peB"], dma="peB")
        load_w_resident(S, wp, W["w_ple"], 2, D, "wp", chunk=D)
        load_w_resident(S, wg, W["w_ple_gate"], 16, D, "wg", chunk=512)
        h_t = h.rearrange("(n p) d -> n p d", p=128)
        p_t = p_in.rearrange("(n p) d -> n p d", p=128)
        tb = Rot([0, 1, 2, 3])
        cnt = 0
        def ple_loads(blk):
            b = blk % 2
            S.add("sp", lambda e: e.dma_start(out=xin[b][:], in_=h_t[blk]),
                  writes=[f"xin{b}"], dma=f"xin{b}")
            S.add("sp", lambda e: e.dma_start(out=pin[b][:], in_=p_t[blk]),
                  writes=[f"pin{b}"], dma=f"pin{b}")
        def ple_nt(blk):
            b = blk % 2
            norm_transpose(S, cx, h_t[blk], xin[b][:], xn[:], ss[:], rstd[:], gB[:], xnT[b][:],
                           ps, tb, b, 16, f"xnT{b}", load=False)
        ple_loads(0)
        ple_nt(0)
        for blk in range(NB):
            b = blk % 2
            if blk + 1 < NB:
                ple_loads(blk + 1)
            S.add("dve", lambda e, b=b: e.tensor_copy(out=pbf[:], in_=pin[b][:]),
                  reads=[f"pin{b}"], writes=["pbf"])
            bank = tb.next()

            def trp(e, bank=bank):
                pst = ps[:, bank, :].bitcast(BF16)
                ins = None
                for c in range(2):
                    ins = e.transpose(out=pst[:, c * 128:(c + 1) * 128],
                                      in_=pbf[:, c * 128:(c + 1) * 128], identity=cx.ident[:])
                return ins
            S.add("pe", trp, reads=["pbf", "ident"], writes=[f"ps{bank}"])
            S.add("dve", lambda e, bank=bank, b=b: e.tensor_copy(
                out=pT[b][:], in_=ps[:, bank, :].bitcast(BF16)[:, 0:256].rearrange(
                    "p (c t) -> p c t", c=2)), reads=[f"ps{bank}"], writes=[f"pT{b}"])

            def me(e, b=b):
                ins = None
                for g in range(4):
                    for c in range(2):
                        ins = e.matmul(out=ps[:, 4 + g, :], lhsT=pT[b][:, c, :],
                                       rhs=wp[:, c, g * 512:(g + 1) * 512], start=(c == 0),
                                       stop=(c == 1))
                return ins
            S.add("pe", me, reads=[f"pT{b}", "wp0"], writes=["ps4", "ps5", "ps6", "ps7"])
            for g in range(4):
                S.add("act", lambda e, g=g: e.activation(out=junk[:], in_=ps[:, 4 + g, :],
                                                         func=AF.Square, accum_out=sse[:, g:g + 1]),
                      reads=[f"ps{4 + g}"], writes=["junk", "sse"])
            S.add("dve", lambda e: e.reduce_sum(out=re_[:], in_=sse[:], axis=AX.X),
                  reads=["sse"], writes=["re"])
            rstd_op(S, cx, re_[:], re_[:], float(D), ["re"], ["re"])
            if blk + 1 < NB:
                ple_nt(blk + 1)
            for g in range(4):
                bank = tb.next()

                def mg(e, b=b, g=g, bank=bank):
                    ins = None
                    for kc in range(16):
                        ins = e.matmul(out=ps[:, bank, :], lhsT=xnT[b][:, kc, :],
                                       rhs=wg[:, kc, g * 512:(g + 1) * 512], start=(kc == 0),
                                       stop=(kc == 15))
                    return ins
                S.add("pe", mg, reads=[f"xnT{b}", f"wg{g}"], writes=[f"ps{bank}"])
                i2 = cnt % 2
                cnt += 1
                S.add("act", lambda e, bank=bank, i2=i2: e.activation(
                    out=sg[i2][:], in_=ps[:, bank, :], func=AF.Sigmoid),
                    reads=[f"ps{bank}"], writes=[f"sg{i2}"])
                S.add("dve", lambda e, g=g, i2=i2: e.scalar_tensor_tensor(
                    out=t1[i2][:], in0=ps[:, 4 + g, :], scalar=re_[:, 0:1],
                    in1=peB[:, g * 512:(g + 1) * 512], op0=ALU.mult, op1=ALU.mult),
                    reads=[f"ps{4 + g}", "re", "peB"], writes=[f"t1{i2}"])
                S.add("dve", lambda e, i2=i2: e.tensor_tensor(out=t1[i2][:], in0=t1[i2][:],
                                                              in1=sg[i2][:], op=ALU.mult),
                      reads=[f"t1{i2}", f"sg{i2}"], writes=[f"t1{i2}"])
                S.add("pool", lambda e, i2=i2, b=b, g=g: e.tensor_tensor(
                    out=xin[b][:, g * 512:(g + 1) * 512], in0=xin[b][:, g * 512:(g + 1) * 512],
                    in1=t1[i2][:], op=ALU.add), reads=[f"t1{i2}", f"xin{b}"], writes=[f"xin{b}"])
            S.add("sp", lambda e, b=b, blk=blk: e.dma_start(out=h_t[blk], in_=xin[b][:]),
                  reads=[f"xin{b}"], dma=f"xins{b}")
        S.emit()


WNAMES = [("ffn_a_norm", (D,)), ("ffn_a_w1", (D, DFF)), ("ffn_a_w3", (D, DFF)), ("ffn_a_w2", (DFF, D)),
          ("mix_norm", (D,)), ("w_in", (D, INC)), ("q_a_norm", (QL,)), ("w_uq", (QL, NH * 192)),
          ("kv_a_norm", (KVL,)), ("w_ukv", (KVL, NH * 256)), ("q_norm", (192,)), ("k_norm", (192,)),
          ("gm_v_norm", (GMW,)), ("gm_ws", (8, 128, 128)), ("gm_bs", (8, 128)),
          ("attn_out_norm", (1024,)), ("gm_out_norm", (GMW,)), ("w_out", (D, D)),
          ("ffn_b_norm", (D,)), ("ffn_b_w1", (D, DFF)), ("ffn_b_w3", (D, DFF)), ("ffn_b_w2", (DFF, D)),
          ("ple_gate_norm", (D,)), ("w_ple_gate", (D, D)), ("w_ple", (PLE, D)), ("ple_norm", (D,))]

ALL_STAGES = ("ffn_a", "zg", "qk", "attn", "wout", "ffn_b", "ple")


def build_program(NTOK=4096, depth=2, stages=ALL_STAGES, T_FFN=1024, debug=False):
    nc = bass.Bass("TRN2", target_bir_lowering=False)
    NB = NTOK // 128

    def dt(name, shape, dtype=F32, kind="ExternalInput"):
        return nc.dram_tensor(name, list(shape), dtype, kind=kind).ap()
    x = dt("x", [NTOK, D])
    p = dt("p", [depth, NTOK, PLE])
    pos = dt("pos_pm", [128, NB], I32)
    invf = dt("inv_freq", [32])
    Wd = {name: dt(name, (depth,) + shp) for name, shp in WNAMES}
    out = dt("out", [NTOK, D], kind="ExternalOutput")
    sk = "ExternalOutput" if debug else "Internal"
    sc = {
        "cos": dt("sc_cos", [128, NB, 32], F32, sk), "sin": dt("sc_sin", [128, NB, 32], F32, sk),
        "cqT": dt("sc_cqT", [4, 128, NTOK], BF16, sk), "ckvT": dt("sc_ckvT", [2, 128, NTOK], BF16, sk),
        "stat": dt("sc_stat", [128, NB, 8], F32, sk), "krr": dt("sc_krr", [128, NB, 64], F32, sk),
        "qTn": dt("sc_qTn", [8, 128, NTOK], BF16, sk), "qTr": dt("sc_qTr", [8, 64, NTOK], BF16, sk),
        "kTn": dt("sc_kTn", [8, 128, NTOK], BF16, sk), "kTr": dt("sc_kTr", [64, NTOK], BF16, sk),
        "ksc": dt("sc_ksc", [128, NB, 8], F32, sk), "mixT": dt("sc_mixT", [16, 128, NTOK], BF16, sk),
    }
    cx = Ctx()
    with ExitStack() as es:
        POOL["es"] = es
        POOL["sems"] = {}
        load_consts(nc, es, cx)
        if "qk" in stages:
            stage_rope(nc, cx, pos, invf, sc, NTOK)
        first = True
        for i in range(depth):
            W = {k: v[i] for k, v in Wd.items()}
            if "ffn_a" in stages:
                stage_ffn(nc, cx, x if first else out, out, W["ffn_a_norm"], W["ffn_a_w1"],
                          W["ffn_a_w3"], W["ffn_a_w2"], NTOK, T_FFN)
                first = False
            src = x if first else out
            if "zg" in stages:
                stage_zg(nc, cx, src, W, sc, NTOK)
            if "qk" in stages:
                stage_qk(nc, cx, W, sc, NTOK)
            if "attn" in stages:
                stage_attn(nc, cx, W, sc, NTOK)
            if "wout" in stages:
                assert not first
                stage_wout(nc, cx, out, W, sc, NTOK)
            if "ffn_b" in stages:
                stage_ffn(nc, cx, out, out, W["ffn_b_norm"], W["ffn_b_w1"], W["ffn_b_w3"],
                          W["ffn_b_w2"], NTOK, T_FFN)
            if "ple" in stages:
                stage_ple(nc, cx, out, p[i], W, NTOK)
    return nc


INV_FREQ = (10000.0 ** (-np.arange(0, 64, 2, dtype=np.float32) / 64)).astype(np.float32)


def make_in_maps(inputs, ncores, NTOK):
    maps = []
    NB = NTOK // 128
    for c in range(ncores):
        m = {"x": np.ascontiguousarray(inputs["x"][c]),
             "p": np.ascontiguousarray(inputs["p"][:, c]),
             "pos_pm": np.ascontiguousarray(inputs["positions"][c].reshape(NB, 128).T),
             "inv_freq": INV_FREQ}
        for name, _ in WNAMES:
            m[name] = inputs[name]
        maps.append(m)
    return maps


_NC_CACHE = {}


def kernel(**inputs):
    B, NTOK, _ = inputs["x"].shape
    depth = inputs["p"].shape[0]
    key = (NTOK, depth)
    if key not in _NC_CACHE:
        _NC_CACHE[key] = build_program(NTOK=NTOK, depth=depth)
    nc = _NC_CACHE[key]
    inputs = {k: np.asarray(v) for k, v in inputs.items()}
    maps = make_in_maps(inputs, B, NTOK)
    res = run_bass_kernel_spmd(nc, maps, core_ids=list(range(B)))
    return np.stack([np.asarray(r["out"], dtype=np.float32) for r in res.results], axis=0)
```
